# Optimizing a Trainium2 kernel written in Bass

```python
import jax, jax.numpy as jnp
from jax import lax
import numpy as np

D_MODEL = 1024
BATCH = 8
SEQ = 4096
DEPTH = 4

GRID_W = 64
CTX_LEN = 256
D_FF = 2816
MIX_W = D_MODEL
A_W = MIX_W // 2
B_W = MIX_W - A_W
GMLP_GROUPS = 4
GMLP_CH = A_W // GMLP_GROUPS
GMLP_CHUNK = 128
HGRN_HEADS = 4
HGRN_K = B_W // HGRN_HEADS
HGRN_V = B_W // HGRN_HEADS
HGRN_CHUNK = 64
IN_W = 2 * A_W + 5 * B_W
N_MOD = 9
EPS = 1e-6
POS_THETA = 10000.0

kernel_name = "hybrid_gmlp_hgrn2_diffusion_trunk"


def _rmsnorm(x, g):
    xf = x.astype(jnp.float32)
    y = xf * lax.rsqrt(jnp.mean(xf * xf, axis=-1, keepdims=True) + EPS)
    return (y * g.astype(jnp.float32)).astype(x.dtype)


def _modulation(cond, w, b):
    m = jax.nn.silu(cond) @ w + b
    return [m[:, None, j * D_MODEL:(j + 1) * D_MODEL] for j in range(N_MOD)]


def _modulate(x, g, shift, scale):
    return _rmsnorm(x, g) * (1 + scale) + shift


def _swiglu(h, w1, w3, w2):
    return (jax.nn.silu(h @ w1) * (h @ w3)) @ w2


def _grid_pos_embed(rows, dtype):
    row = jnp.repeat(jnp.arange(rows), GRID_W).astype(jnp.float32)
    col = jnp.tile(jnp.arange(GRID_W), rows).astype(jnp.float32)
    quarter = D_MODEL // 4
    freq = POS_THETA ** (-jnp.arange(quarter, dtype=jnp.float32) / quarter)

    def axis_embed(p):
        a = p[:, None] * freq[None, :]
        return jnp.concatenate([jnp.sin(a), jnp.cos(a)], axis=-1)

    return jnp.concatenate([axis_embed(row), axis_embed(col)], axis=-1).astype(dtype)


def _gmlp_spatial(u, v, w_s, b_s, g_v):
    B, L, _ = u.shape
    n = L // GMLP_CHUNK
    u = jax.nn.gelu(u).reshape(B, n, GMLP_CHUNK, GMLP_GROUPS, GMLP_CH)
    v = _rmsnorm(jax.nn.gelu(v).reshape(B, L, GMLP_GROUPS, GMLP_CH), g_v.reshape(GMLP_GROUPS, GMLP_CH))
    v = v.reshape(B, n, GMLP_CHUNK, GMLP_GROUPS, GMLP_CH)
    sv = jnp.einsum("gts,bnsgc->bntgc", w_s, v) + b_s.T[:, :, None]
    return (u * sv).reshape(B, L, A_W)


def _lower_bounds(p):
    cum = jnp.cumsum(jax.nn.softmax(p.astype(jnp.float32), axis=0), axis=0)
    return cum - cum[0:1]


def _forget(f_raw, lb):
    f_raw = f_raw.astype(jnp.float32)
    log_f = jnp.logaddexp(jnp.log(lb), jnp.log1p(-lb) + jax.nn.log_sigmoid(f_raw))
    k = (1.0 - lb) * jax.nn.sigmoid(-f_raw)
    return k, log_f


def _gla_scan(q, k, log_f, v, s0):
    B, L, H, K = q.shape
    V = v.shape[-1]
    n = L // HGRN_CHUNK

    def to_chunks(a):
        return a.reshape(B, n, HGRN_CHUNK, H, a.shape[-1]).transpose(1, 0, 3, 2, 4)

    mask = jnp.tril(jnp.ones((HGRN_CHUNK, HGRN_CHUNK), dtype=bool))[:, :, None]

    def step(S, inp):
        qc, kc, gc, vc = inp
        b = jnp.cumsum(gc, axis=2)
        diff = b[:, :, :, None, :] - b[:, :, None, :, :]
        decay = jnp.exp(jnp.where(mask, diff, -jnp.inf))
        scores = jnp.einsum("bhtk,bhsk,bhtsk->bhts", qc, kc, decay)
        o = (jnp.einsum("bhts,bhsv->bhtv", scores, vc)
             + jnp.einsum("bhtk,bhkv->bhtv", qc * jnp.exp(b), S))
        b_end = b[:, :, -1, :]
        S = (jnp.exp(b_end)[..., None] * S
             + jnp.einsum("bhsk,bhsv->bhkv", kc * jnp.exp(b_end[:, :, None, :] - b), vc))
        return S, o

    s_fin, o = lax.scan(step, s0, (to_chunks(q), to_chunks(k), to_chunks(log_f), to_chunks(v)))
    return o.transpose(1, 0, 3, 2, 4).reshape(B, L, H, V), s_fin


def _hgrn_bidir(q, f_fw, f_bw, i, lb_f, lb_b, s0_f, s0_b):
    B, L, _ = q.shape

    def heads(a):
        return a.astype(jnp.float32).reshape(B, L, HGRN_HEADS, -1)

    def flip(a):
        return a[:, ::-1]

    qh = heads(jax.nn.silu(q))
    vh = heads(i)
    kf, gf = _forget(f_fw, lb_f)
    kb, gb = _forget(f_bw, lb_b)
    o_f, s_f = _gla_scan(qh, heads(kf), heads(gf), vh, s0_f)
    o_b, s_b = _gla_scan(flip(qh), flip(heads(kb)), flip(heads(gb)), flip(vh), s0_b)
    return o_f + flip(o_b), s_f, s_b


def _token_mixer(h, w_in, w_out, w_s, b_s, g_v, g_o, lb_f, lb_b, s0_f, s0_b):
    B, L, _ = h.shape
    z = h @ w_in
    cuts = [A_W, 2 * A_W, 2 * A_W + B_W, 2 * A_W + 2 * B_W, 2 * A_W + 3 * B_W, 2 * A_W + 4 * B_W]
    u, v, q, f_fw, f_bw, i, og = jnp.split(z, cuts, axis=-1)
    a_out = _gmlp_spatial(u, v, w_s, b_s, g_v)
    o, s_f, s_b = _hgrn_bidir(q, f_fw, f_bw, i, lb_f, lb_b, s0_f, s0_b)
    o = _rmsnorm(o, g_o.reshape(HGRN_HEADS, HGRN_V)).reshape(B, L, B_W).astype(h.dtype)
    b_out = o * jax.nn.silu(og)
    return jnp.concatenate([a_out, b_out], axis=-1) @ w_out, s_f, s_b


def _context_states(h, w_in, lb_f, lb_b, s0_f, s0_b):
    z = h @ w_in[:, 2 * A_W:2 * A_W + 4 * B_W]
    q, f_fw, f_bw, i = jnp.split(z, 4, axis=-1)
    _, s_f, s_b = _hgrn_bidir(q, f_fw, f_bw, i, lb_f, lb_b, s0_f, s0_b)
    return s_f, s_b


def setup_inputs(seed: int = 0) -> dict:
    key = jax.random.key(seed)
    ks = jax.random.split(key, 24)

    def nrm(k, shape, s):
        return jax.random.normal(k, shape, jnp.float32) * s

    def gain(k, shape):
        return 1.0 + 0.02 * jax.random.normal(k, shape, jnp.float32)

    d_s = D_MODEL ** -0.5
    return {
        "x": nrm(ks[0], (BATCH, SEQ, D_MODEL), 1.0),
        "c": nrm(ks[1], (BATCH, D_MODEL), 1.0),
        "ctx": nrm(ks[2], (BATCH, CTX_LEN, D_MODEL), 1.0),
        "c_ctx": nrm(ks[3], (D_MODEL,), 1.0),
        "w_ada": nrm(ks[4], (DEPTH, D_MODEL, N_MOD * D_MODEL), d_s),
        "b_ada": nrm(ks[5], (DEPTH, N_MOD * D_MODEL), 0.02),
        "norm_ffn1_g": gain(ks[6], (DEPTH, D_MODEL)),
        "norm_mix_g": gain(ks[7], (DEPTH, D_MODEL)),
        "norm_ffn2_g": gain(ks[8], (DEPTH, D_MODEL)),
        "ffn1_w1": nrm(ks[9], (DEPTH, D_MODEL, D_FF), d_s),
        "ffn1_w3": nrm(ks[10], (DEPTH, D_MODEL, D_FF), d_s),
        "ffn1_w2": nrm(ks[11], (DEPTH, D_FF, D_MODEL), D_FF ** -0.5),
        "ffn2_w1": nrm(ks[12], (DEPTH, D_MODEL, D_FF), d_s),
        "ffn2_w3": nrm(ks[13], (DEPTH, D_MODEL, D_FF), d_s),
        "ffn2_w2": nrm(ks[14], (DEPTH, D_FF, D_MODEL), D_FF ** -0.5),
        "w_in": nrm(ks[15], (DEPTH, D_MODEL, IN_W), d_s),
        "w_out": nrm(ks[16], (DEPTH, MIX_W, D_MODEL), MIX_W ** -0.5),
        "gmlp_ws": nrm(ks[17], (DEPTH, GMLP_GROUPS, GMLP_CHUNK, GMLP_CHUNK), GMLP_CHUNK ** -0.5),
        "gmlp_bs": gain(ks[18], (DEPTH, GMLP_GROUPS, GMLP_CHUNK)),
        "gmlp_norm_g": gain(ks[19], (DEPTH, A_W)),
        "hgrn_lb_fwd": nrm(ks[20], (DEPTH, B_W), 0.5),
        "hgrn_lb_bwd": nrm(ks[21], (DEPTH, B_W), 0.5),
        "hgrn_norm_g": gain(ks[22], (DEPTH, B_W)),
        "norm_final_g": gain(ks[23], (D_MODEL,)),
    }


def reference(x, c, ctx, c_ctx, w_ada, b_ada, norm_ffn1_g, norm_mix_g, norm_ffn2_g,
              ffn1_w1, ffn1_w3, ffn1_w2, ffn2_w1, ffn2_w3, ffn2_w2, w_in, w_out,
              gmlp_ws, gmlp_bs, gmlp_norm_g, hgrn_lb_fwd, hgrn_lb_bwd, hgrn_norm_g, norm_final_g):
    B, L, _ = x.shape
    rows = L // GRID_W
    lat = x + _grid_pos_embed(rows, x.dtype)[None]
    cx = ctx
    lbs_f = _lower_bounds(hgrn_lb_fwd)
    lbs_b = _lower_bounds(hgrn_lb_bwd)
    zero_state = jnp.zeros((B, HGRN_HEADS, HGRN_K, HGRN_V), jnp.float32)

    for l in range(DEPTH):
        last = l == DEPTH - 1
        ml = _modulation(c, w_ada[l], b_ada[l])
        mc = _modulation(c_ctx[None], w_ada[l], b_ada[l])

        ffn1 = (ffn1_w1[l], ffn1_w3[l], ffn1_w2[l])
        lat = lat + 0.5 * ml[2] * _swiglu(_modulate(lat, norm_ffn1_g[l], ml[0], ml[1]), *ffn1)
        cx = cx + 0.5 * mc[2] * _swiglu(_modulate(cx, norm_ffn1_g[l], mc[0], mc[1]), *ffn1)

        mix_w = (w_in[l], w_out[l], gmlp_ws[l], gmlp_bs[l], gmlp_norm_g[l], hgrn_norm_g[l],
                 lbs_f[l], lbs_b[l])
        h_cx = _modulate(cx, norm_mix_g[l], mc[3], mc[4])
        if last:
            s_f, s_b = _context_states(h_cx, w_in[l], lbs_f[l], lbs_b[l], zero_state, zero_state)
        else:
            out_cx, s_f, s_b = _token_mixer(h_cx, *mix_w, zero_state, zero_state)
            cx = cx + mc[5] * out_cx
        out_lat, _, _ = _token_mixer(_modulate(lat, norm_mix_g[l], ml[3], ml[4]), *mix_w, s_f, s_b)
        lat = lat + ml[5] * out_lat

        ffn2 = (ffn2_w1[l], ffn2_w3[l], ffn2_w2[l])
        lat = lat + 0.5 * ml[8] * _swiglu(_modulate(lat, norm_ffn2_g[l], ml[6], ml[7]), *ffn2)
        if not last:
            cx = cx + 0.5 * mc[8] * _swiglu(_modulate(cx, norm_ffn2_g[l], mc[6], mc[7]), *ffn2)

    return _rmsnorm(lat, norm_final_g)
```

```python
import numpy as np
import concourse.bass as bass
import concourse.mybir as mybir
from concourse.ap import AP
from concourse.bass_utils import run_bass_kernel_spmd

F32 = mybir.dt.float32
BF16 = mybir.dt.bfloat16
AF = mybir.ActivationFunctionType
ALU = mybir.AluOpType
AX = mybir.AxisListType

D = 1024
DC = 8
DFF = 2816
FCN = 22
INW = 3584
NCTX = 256
GRID_W = 64
EPS = 1e-6
CAP = 80.0
TT = 512


class V:
    __slots__ = ("ap", "keys")

    def __init__(self, ap, keys):
        self.ap = ap
        self.keys = keys


class Buf:
    def __init__(self, t, name):
        self.t = t
        self.name = name

    def __getitem__(self, idx):
        return V(self.t[idx], (self.name,))

    def v(self, ap):
        return V(ap, (self.name,))


def vkeys(x):
    return x.keys if isinstance(x, V) else ()


def vap(x):
    return x.ap if isinstance(x, V) else x


def bc(ap, pattern):
    return AP(ap.tensor, ap.offset, [list(ap.ap[0])] + [list(p) for p in pattern])


class Prog:
    ENG = ("pe", "act", "dve", "pool", "sp")

    def __init__(self, nc):
        self.nc = nc
        self.e = {"pe": nc.tensor, "act": nc.scalar, "dve": nc.vector, "pool": nc.gpsimd, "sp": nc.sync}
        self.stack = []
        self.sems = {}
        self.cnt = {}
        self.seen = {e: {} for e in self.ENG}
        self.last_w = {}
        self.readers = {}
        self.ninst = 0
        self.nbank = 0
        self.epoch = 0
        self.esem = {}
        for e in self.ENG:
            self.esem[e] = self.new_sem("E_" + e)

    def new_sem(self, name):
        cm = self.nc.semaphore(name)
        s = cm.__enter__()
        self.stack.append(cm)
        self.sems[name] = s
        self.cnt[name] = 0
        return name

    def sbuf(self, name, shape, dtype):
        cm = self.nc.sbuf_tensor(name, list(shape), dtype)
        t = cm.__enter__()
        self.stack.append(cm)
        return Buf(t, name)

    def psum(self, name, shape, dtype=F32):
        cm = self.nc.psum_tensor(name, list(shape), dtype)
        t = cm.__enter__()
        self.stack.append(cm)
        return Buf(t, name)

    def dram(self, name, shape, dtype, kind="Internal"):
        t = self.nc.dram_tensor(name, list(shape), dtype, kind=kind)
        return Buf(t.ap(), name)

    def close(self):
        for cm in reversed(self.stack):
            cm.__exit__(None, None, None)
        self.stack = []

    def _need(self, eng, tok):
        kind, sem = tok[0], tok[1]
        val = tok[2]
        if kind != "dma" and sem == self.esem[eng] and eng == "pe":
            return
        if self.seen[eng].get(sem, 0) >= val:
            return
        self.e[eng].wait_ge(self.sems[sem], val)
        self.seen[eng][sem] = val

    def _deps(self, eng, reads, writes):
        for k in reads:
            t = self.last_w.get(k)
            if t is not None:
                self._need(eng, t)
        for k in writes:
            t = self.last_w.get(k)
            if t is not None:
                self._need(eng, t)
            for t in self.readers.get(k, ()):
                self._need(eng, t)

    def _record(self, tok, reads, writes):
        for k in writes:
            self.last_w[k] = tok
            self.readers[k] = []
        for k in reads:
            if k in writes:
                continue
            lst = self.readers.setdefault(k, [])
            lst[:] = [t for t in lst if t[1] != tok[1]]
            lst.append(tok)

    def op(self, eng, fn, outs, ins):
        writes = [k for o in outs for k in vkeys(o)]
        reads = [k for i in ins for k in vkeys(i)]
        if self.cnt[self.esem[eng]] >= 30000:
            self.barrier()
            self.epoch += 1
            self.esem[eng] = self.new_sem("E_%s_%d" % (eng, self.epoch))
        self._deps(eng, reads, writes)
        inst = fn()
        sem = self.esem[eng]
        self.cnt[sem] += 1
        inst.then_inc(self.sems[sem], 1)
        self._record(("eng", sem, self.cnt[sem]), reads, writes)
        self.ninst += 1
        return inst

    def dma(self, q, out, in_, sem, **kw):
        writes = list(vkeys(out))
        reads = list(vkeys(in_))
        self._deps(q, reads, writes)
        inst = self.e[q].dma_start(out=vap(out), in_=vap(in_), **kw)
        inst.then_inc(self.sems[sem], 16)
        self.cnt[sem] += 1
        self._record(("dma", sem, 16 * self.cnt[sem]), reads, writes)
        self.ninst += 1
        return inst

    def _all_tokens(self, skip_prefix=None):
        toks = set()
        for name, c in self.cnt.items():
            if c == 0:
                continue
            if skip_prefix and name.startswith(skip_prefix):
                continue
            if name.startswith("E_"):
                toks.add(("eng", name, c))
            else:
                toks.add(("dma", name, 16 * c))
        return toks

    def wait_all(self, eng, skip_prefix=None):
        for t in self._all_tokens(skip_prefix):
            self._need(eng, t)

    def barrier(self, skip_prefix=None):
        for e in self.ENG:
            self.wait_all(e, skip_prefix)

    def mm(self, out, lhsT, rhs, start=True, stop=True, **kw):
        return self.op("pe", lambda: self.nc.tensor.matmul(vap(out), vap(lhsT), vap(rhs), start=start, stop=stop, **kw),
                       [out], [lhsT, rhs])

    def transpose(self, out, in_, ident):
        return self.op("pe", lambda: self.nc.tensor.transpose(vap(out), vap(in_), vap(ident)), [out], [in_, ident])

    def act(self, out, in_, func, bias=0.0, scale=1.0, accum_out=None):
        outs = [out] + ([accum_out] if accum_out is not None else [])
        kw = {}
        if accum_out is not None:
            kw["accum_out"] = vap(accum_out)
        return self.op("act", lambda: self.nc.scalar.activation(out=vap(out), in_=vap(in_), func=func, bias=vap(bias),
                                                                 scale=vap(scale), **kw),
                       outs, [in_, bias, scale])

    def tt(self, eng, out, in0, in1, op):
        return self.op(eng, lambda: self.e[eng].tensor_tensor(out=vap(out), in0=vap(in0), in1=vap(in1), op=op),
                       [out], [in0, in1])

    def ts(self, eng, out, in0, s1, op0, s2=None, op1=None):
        def fn():
            if op1 is None:
                return self.e[eng].tensor_scalar(out=vap(out), in0=vap(in0), scalar1=vap(s1), scalar2=None, op0=op0)
            return self.e[eng].tensor_scalar(out=vap(out), in0=vap(in0), scalar1=vap(s1), scalar2=vap(s2), op0=op0, op1=op1)
        return self.op(eng, fn, [out], [in0, s1, s2])

    def stt(self, out, in0, scalar, in1, op0, op1):
        return self.op("dve", lambda: self.nc.vector.scalar_tensor_tensor(out=vap(out), in0=vap(in0), scalar=vap(scalar),
                                                                           in1=vap(in1), op0=op0, op1=op1),
                       [out], [in0, scalar, in1])

    def scan(self, out, d0, d1, initial, op0, op1):
        return self.op("dve", lambda: self.nc.vector.tensor_tensor_scan(out=vap(out), data0=vap(d0), data1=vap(d1),
                                                                         initial=vap(initial), op0=op0, op1=op1),
                       [out], [d0, d1, initial])

    def copy(self, eng, out, in_):
        if eng == "act":
            return self.op("act", lambda: self.nc.scalar.copy(out=vap(out), in_=vap(in_)), [out], [in_])
        return self.op(eng, lambda: self.e[eng].tensor_copy(out=vap(out), in_=vap(in_)), [out], [in_])

    def recip(self, out, in_):
        return self.op("dve", lambda: self.nc.vector.reciprocal(out=vap(out), in_=vap(in_)), [out], [in_])

    def reduce(self, out, in_, op):
        return self.op("dve", lambda: self.nc.vector.tensor_reduce(out=vap(out), in_=vap(in_), axis=AX.X, op=op),
                       [out], [in_])

    def memset(self, eng, out, val):
        return self.op(eng, lambda: self.e[eng].memset(vap(out), val), [out], [])


class Ring:
    def __init__(self, P, name, n, shape, dtype):
        self.bufs = [P.sbuf(f"{name}{i}", shape, dtype) for i in range(n)]
        self.sems = [P.new_sem(f"S_{name}{i}") for i in range(n)]
        self.ssem = {b.name: P.new_sem(f"T_{name}{i}") for i, b in enumerate(self.bufs)}
        self.n = n
        self.i = 0

    def next(self):
        j = self.i % self.n
        self.i += 1
        return self.bufs[j], self.sems[j]


def run_jobs(jobs, depth):
    handles = {}
    n = len(jobs)
    for j in range(min(depth - 1, n)):
        handles[j] = jobs[j][0]() if jobs[j][0] else None
    for i in range(n):
        j = i + depth - 1
        if j < n:
            handles[j] = jobs[j][0]() if jobs[j][0] else None
        jobs[i][1](handles.pop(i))


def spar_layout(depth):
    off = {}
    o = 0
    for name, n in (("g", depth * 24), ("gfin", 8), ("go", depth * 4), ("lbf", 4 * depth), ("lbb", 4 * depth),
                    ("bada", depth * 72)):
        off[name] = o
        o += n
    return off, o


C_M64, C_M16, C_MF, C_MB, C_ID, C_ONE, C_N = 0, 512, 1024, 1536, 2048, 2176, 2304


def make_consts():
    c = np.zeros((128, C_N), np.float32)
    t = np.arange(512)
    c[:, C_M64:C_M64 + 512] = (t % 64 != 0).astype(np.float32)[None]
    c[:, C_M16:C_M16 + 512] = (t % 16 != 0).astype(np.float32)[None]
    s = np.arange(128)[:, None]
    tt_ = np.arange(128)[None, :]
    same = (s // 64) == (tt_ // 64)
    mf = (same & (s <= tt_)).astype(np.float32)
    mb = (same & (s >= tt_)).astype(np.float32)
    c[:, C_MF:C_MF + 512] = np.tile(mf, (1, 4))
    c[:, C_MB:C_MB + 512] = np.tile(mb, (1, 4))
    c[:, C_ID:C_ID + 128] = np.eye(128, dtype=np.float32)
    c[:, C_ONE:C_ONE + 128] = 1.0
    return c


def pos_table(L):
    rows = L // GRID_W
    row = np.repeat(np.arange(rows), GRID_W).astype(np.float32)
    col = np.tile(np.arange(GRID_W), rows).astype(np.float32)
    quarter = D // 4
    freq = (np.float32(10000.0) ** (-np.arange(quarter, dtype=np.float32) / np.float32(quarter))).astype(np.float32)

    def axis_embed(p):
        a = (p[:, None] * freq[None, :]).astype(np.float32)
        return np.concatenate([np.sin(a), np.cos(a)], axis=-1)

    pe = np.concatenate([axis_embed(row), axis_embed(col)], axis=-1).astype(np.float32)
    return fm(pe)


def fm(a):
    n = a.shape[0]
    return np.ascontiguousarray(a.T.reshape(DC, 128, n).transpose(1, 0, 2))


def fmv(v):
    k = v.shape[-1] // 128
    r = v.reshape(v.shape[:-1] + (k, 128))
    return np.moveaxis(r, -1, 0)


def build(L, depth, stop_after=None):
    nc = bass.Bass("TRN2", target_bir_lowering=False)
    P = Prog(nc)
    NT = L // TT
    NTOK = L + NCTX
    soff, NSP = spar_layout(depth)
    NBP = 512 + 2048

    xT = P.dram("xT", [128, DC, L], F32, "ExternalInput")
    posT = P.dram("posT", [128, DC, L], F32, "ExternalInput")
    ctxT = P.dram("ctxT", [128, DC, NCTX], F32, "ExternalInput")
    cvd = P.dram("cv", [128, DC, 2], F32, "ExternalInput")
    spard = P.dram("spar", [128, NSP], F32, "ExternalInput")
    bpard = P.dram("bpar", [depth, 128, NBP], F32, "ExternalInput")
    wsTd = P.dram("wsT", [128, depth * 512], F32, "ExternalInput")
    cstd = P.dram("cst", [128, C_N], F32, "ExternalInput")
    w_ada = P.dram("w_ada", [depth, D, 9 * D], F32, "ExternalInput")
    wnames = [("f1w1", D, DFF), ("f1w3", D, DFF), ("f1w2", DFF, D), ("f2w1", D, DFF), ("f2w3", D, DFF),
              ("f2w2", DFF, D), ("win", D, INW), ("wout", D, D)]
    wf = {n: P.dram(n, [depth, r, c], F32, "ExternalInput") for n, r, c in wnames}
    wb = {n: P.dram(n + "_b", [depth, r, c], BF16) for n, r, c in wnames}
    outT = P.dram("outT", [128, DC, L], F32, "ExternalOutput")
    xs = P.dram("xs", [128, DC, NTOK], F32)
    obd = P.dram("obd", [128, 4, NTOK], F32)

    def xs_v(t0, T):
        return V(xs.t[:, :, t0:t0 + T], (("xs", t0),))

    def ob_v(t0, T):
        return V(obd.t[:, :, t0:t0 + T], (("ob", t0),))

    spar = P.sbuf("spar_t", [128, NSP], F32)
    bpar = P.sbuf("bpar_t", [128, NBP], F32)
    wsb = P.sbuf("wsb", [128, depth * 512], BF16)
    cst = P.sbuf("cst_t", [128, C_N], F32)
    identb = P.sbuf("identb", [128, 128], BF16)
    onesb = P.sbuf("onesb", [128, 128], BF16)
    cv = P.sbuf("cv_t", [128, DC, 2], F32)
    scv = P.sbuf("scv", [128, DC, 2], F32)
    mod = P.sbuf("mod", [128, 72, 2], F32)
    md = P.sbuf("md", [128, 9, DC, 2], F32)
    lbt = [P.sbuf(f"lbt{d}", [128, 4, depth], F32) for d in range(2)]
    omlt = [P.sbuf(f"omlt{d}", [128, 4, depth], F32) for d in range(2)]
    lbtmp = P.sbuf("lbtmp", [128, 4, depth], F32)
    lbm = P.sbuf("lbm", [128, 4], F32)
    Sst = [[P.sbuf(f"Sst{d}{h}", [128, 128], F32) for h in range(4)] for d in range(2)]
    slabs = Ring(P, "slab", 3, [128, 4096], BF16)
    xring = Ring(P, "xt", 2, [128, DC, TT], F32)
    hT = P.sbuf("hT", [128, DC, TT], BF16)
    sqb = P.sbuf("sqb", [128, 2, TT], BF16)
    tmpn = P.sbuf("tmpn", [128, TT], F32)
    tmpn_b = P.sbuf("tmpn_b", [128, TT], F32)
    tmpn2 = [tmpn, tmpn_b]
    rr = P.sbuf("rr", [128, TT], F32)
    rstd = P.sbuf("rstd", [128, TT], F32)
    arF = P.sbuf("arF", [128, 13824], F32)
    arB = P.sbuf("arB", [128, 22784], BF16)
    banks = [P.psum(f"pb{i}", [128, 512], F32) for i in range(6)]
    psS = P.psum("psS", [128, 512], F32)
    psT = P.psum("psT", [128, 1024], BF16)

    def bank():
        b = banks[P.nbank % 6]
        P.nbank += 1
        return b

    class Arena:
        def __init__(self, buf):
            self.buf = buf
            self.off = 0
            self.gen = 0

        def reset(self):
            self.off = 0
            self.gen += 1

        def carve(self, name, cols):
            b = Buf(self.buf.t[:, self.off:self.off + cols], f"{name}@{self.gen}")
            self.off += cols
            assert self.off <= self.buf.t.shape[1], (name, self.off)
            return b

    AFa = Arena(arF)
    ABa = Arena(arB)

    def new_phase():
        P.barrier("S_wc")
        AFa.reset()
        ABa.reset()

    s_m = [P.new_sem(f"S_misc{i}") for i in range(6)]
    s_ot = P.new_sem("S_ot")
    s_ob = P.new_sem("S_ob")
    s_obl = P.new_sem("S_obl")
    s_wa = [P.new_sem("S_wa0"), P.new_sem("S_wa1")]
    s_wc = [[P.new_sem(f"S_wc{l}{g}") for g in "AB"] for l in range(depth)]

    def wgrp(n):
        return 0 if n.startswith("f1") else 1

    def convert(l, grp):
        for n, r, c in wnames:
            if wgrp(n) != grp:
                continue
            for r0 in range(0, r, 128):
                P.dma("pool", V(wb[n].t[l, r0:r0 + 128, :], (("wb", l, grp),)), V(wf[n].t[l, r0:r0 + 128, :], ()),
                      s_wc[l][grp], max_dma_last_dim=4096)

    convert(0, 0)
    convert(0, 1)

    def finish():
        P.barrier()
        P.close()
        return nc, P

    if stop_after == "conv":
        return finish()
    P.dma("sp", spar[:], spard[:], s_m[0])
    wsf = AFa.carve("wsf", depth * 512)
    P.dma("sp", wsf[:], wsTd[:], s_m[1])
    P.dma("sp", cst[:], cstd[:], s_m[2])
    P.dma("sp", cv[:], cvd[:], s_m[3])
    P.copy("dve", wsb[:], wsf[:])
    P.copy("dve", identb[:], cst[:, C_ID:C_ID + 128])
    P.copy("dve", onesb[:], cst[:, C_ONE:C_ONE + 128])
    P.memset("dve", psS[:], 0.0)
    P.act(scv[:], cv[:], AF.Silu)

    for d, nm in ((0, "lbf"), (1, "lbb")):
        raw = V(spar.t[:, soff[nm]:soff[nm] + 4 * depth].rearrange("p (h l) -> p h l", l=depth), spar[:].keys)
        P.reduce(lbm[:], raw, ALU.max)
        mb_ = V(bc(lbm.t[:, 0:1], [[1, 4], [0, depth]]), lbm[:].keys)
        P.tt("dve", lbtmp[:], raw, mb_, ALU.subtract)
        P.act(lbtmp[:], lbtmp[:], AF.Exp)
        P.reduce(lbm[:], lbtmp[:], ALU.add)
        P.recip(lbm[:], lbm[:])
        P.tt("dve", lbtmp[:], lbtmp[:], mb_, ALU.mult)
        P.memset("dve", lbt[d][:, :, 0:1], 0.0)
        for l in range(1, depth):
            P.tt("dve", lbt[d][:, :, l:l + 1], lbt[d][:, :, l - 1:l], lbtmp[:, :, l:l + 1], ALU.add)
        P.ts("dve", omlt[d][:], lbt[d][:], -1.0, ALU.mult, 1.0, ALU.add)

    if stop_after == "lb":
        return finish()
    tiles = [(i * TT, TT, 0) for i in range(NT)] + [(L, NCTX, 1)]

    P.dma("sp", V(xs.t[:, :, L:L + NCTX], (("xs", L),)), ctxT[:], s_m[4])
    for (t0, T, col) in tiles[:NT]:
        xa, sa = xring.next()
        xb_, sb_ = xring.next()
        P.dma("sp", xa[:], V(xT.t[:, :, t0:t0 + T], ()), sa)
        P.dma("sp", xb_[:], V(posT.t[:, :, t0:t0 + T], ()), sb_)
        P.tt("dve", xa[:], xa[:], xb_[:], ALU.add)
        P.dma("pool", xs_v(t0, T), xa[:], xring.ssem[xa.name])

    if stop_after == "pro":
        return finish()

    def mdv(j, col, kc=None):
        if kc is None:
            return V(md.t[:, j, :, col], md[:].keys)
        return V(md.t[:, j, kc, col:col + 1], md[:].keys)

    def sp_col(name, idx):
        o = soff[name] + idx
        return V(spar.t[:, o:o + 1], spar[:].keys)

    def norm_h(xt, T, gs_j, sh_j, col):
        pb = bank()
        for kc in range(DC):
            sq_ = V(sqb.t[:, kc % 2, :T], (("sqb", kc % 2),))
            if kc % 2 == 0:
                P.act(sq_, xt[:, kc, :T], AF.Square)
            else:
                P.tt("dve", sq_, xt[:, kc, :T], xt[:, kc, :T], ALU.mult)
            P.mm(pb[:, :T], onesb[:], sq_, start=(kc == 0), stop=(kc == DC - 1))
        P.act(rr[:, :T], pb[:, :T], AF.Ln, bias=eps_t[:, 0:1], scale=1.0 / D)
        P.act(rstd[:, :T], rr[:, :T], AF.Exp, scale=-0.5)
        for kc in range(DC):
            tm = tmpn2[kc % 2]
            P.tt("dve", tm[:, :T], xt[:, kc, :T], rstd[:, :T], ALU.mult)
            if gs_j is None:
                continue
            P.act(hT[:, kc, :T], tm[:, :T], AF.Identity, bias=mdv(sh_j, col, kc), scale=mdv(gs_j, col, kc))

    eps_t = P.sbuf("eps_t", [128, 2], F32)
    P.memset("dve", eps_t[:, 0:1], EPS)
    P.memset("dve", eps_t[:, 1:2], 1.0)

    def slab_load(src_ap, view_fn, l):
        def ld():
            sb, ss = slabs.next()
            P.dma("sp", V(view_fn(sb.t), (sb.name,)), V(src_ap, (("wb", l, 1),)), ss)
            return sb
        return ld

    def modulation(l):
        new_phase()
        wsl = [AFa.carve(f"wada{i}", DC * 384) for i in range(2)]
        pm = bank()
        P.dma("sp", bpar[:], V(bpard.t[l], ()), s_m[5])
        for s in range(24):
            wa = wsl[s % 2]
            wav = V(wa.t[:, :].rearrange("p (k n) -> p k n", n=384), (wa.name,))
            P.dma("sp", wav, V(w_ada.t[l, :, s * 384:(s + 1) * 384].rearrange("(k p) n -> p k n", p=128), ()), s_wa[s % 2])
            for jj in range(3):
                jc = s * 3 + jj
                for kc in range(DC):
                    P.mm(pm[:, jc * 2:jc * 2 + 2], V(wa.t[:, kc * 384 + jj * 128:kc * 384 + (jj + 1) * 128], (wa.name,)),
                         scv[:, kc, :], start=(kc == 0), stop=(kc == DC - 1))
        bo = soff["bada"] + l * 72
        P.tt("dve", mod[:], V(pm.t[:, 0:144].rearrange("p (j c) -> p j c", c=2), (pm.name,)),
             V(bc(spar.t[:, bo:bo + 1], [[1, 72], [0, 2]]), spar[:].keys), ALU.add)
        for sub in range(3):
            gcol = soff["g"] + l * 24 + sub * 8
            for col in range(2):
                g_ = V(spar.t[:, gcol:gcol + 8], spar[:].keys)
                P.copy("dve", mdv(sub * 3 + 0, col), V(mod.t[:, (sub * 3) * 8:(sub * 3) * 8 + 8, col], mod[:].keys))
                P.stt(mdv(sub * 3 + 1, col), V(mod.t[:, (sub * 3 + 1) * 8:(sub * 3 + 1) * 8 + 8, col], mod[:].keys), 1.0, g_,
                      ALU.add, ALU.mult)
                P.ts("dve", mdv(sub * 3 + 2, col), V(mod.t[:, (sub * 3 + 2) * 8:(sub * 3 + 2) * 8 + 8, col], mod[:].keys),
                     1.0 if sub == 1 else 0.5, ALU.mult)

    def ffn_pass(l, which, tlist):
        new_phase()
        w1, w3, w2 = (wb[f"f{which}w1"], wb[f"f{which}w3"], wb[f"f{which}w2"])
        wg = 0 if which == 1 else 1
        jsh, jgs, jhg = ((0, 1, 2) if which == 1 else (6, 7, 8))
        aT = ABa.carve("aT", FCN * TT)
        stmp = [AFa.carve(f"stmp{i}", TT) for i in range(2)]
        jobs = []
        state = {}

        for ti, (t0, T, col) in enumerate(tlist):
            def first_load(t0=t0, T=T):
                xt, sx = xring.next()
                P.dma("sp", xt[:, :, :T], xs_v(t0, T), sx)
                return xt

            def stage1(fb2, t0=t0, T=T, col=col, first=False, fl=first_load):
                def ld():
                    xt = fl() if first else None
                    sb, ss = slabs.next()
                    for m, w in enumerate((w1, w3)):
                        dst = sb.t[:, m * 2048:(m + 1) * 2048].rearrange("p (k n) -> p k n", n=256)
                        src = w.t[l, :, fb2 * 256:(fb2 + 1) * 256].rearrange("(k p) n -> p k n", p=128)
                        P.dma("sp", V(dst, (sb.name,)), V(src, (("wb", l, wg),)), ss)
                    return (xt, sb)

                def cp(hd):
                    xt, sb = hd
                    if first:
                        state["xt"] = xt
                        norm_h(xt, T, jgs, jsh, col)
                    for half in range(2):
                        fblk = fb2 * 2 + half
                        p1, p3 = bank(), bank()
                        for m, pbk in ((0, p1), (1, p3)):
                            for kc in range(DC):
                                o = m * 2048 + kc * 256 + half * 128
                                P.mm(pbk[:, :T], V(sb.t[:, o:o + 128], (sb.name,)), hT[:, kc, :T],
                                     start=(kc == 0), stop=(kc == DC - 1))
                        st = stmp[fblk % 2]
                        P.act(st[:, :T], p1[:, :T], AF.Silu)
                        P.tt("dve", V(aT.t[:, fblk * TT:fblk * TT + T], (aT.name,)), st[:, :T], p3[:, :T], ALU.mult)
                return (ld, cp)

            def stage2(dh, fg, t0=t0, T=T, col=col):
                f0 = fg * 8
                nf = min(8, FCN - f0)

                def ld():
                    sb, ss = slabs.next()
                    dst = sb.t[:, :nf * 512].rearrange("p (k n) -> p k n", n=512)
                    src = w2.t[l, f0 * 128:(f0 + nf) * 128, dh * 512:(dh + 1) * 512].rearrange("(k p) n -> p k n", p=128)
                    P.dma("sp", V(dst, (sb.name,)), V(src, (("wb", l, wg),)), ss)
                    return sb

                def cp(sb):
                    if fg == 0:
                        state["acc"] = [bank() for _ in range(4)]
                    acc = state["acc"]
                    for fl in range(nf):
                        fc = f0 + fl
                        for j in range(4):
                            P.mm(acc[j][:, :T], V(sb.t[:, fl * 512 + j * 128:fl * 512 + (j + 1) * 128], (sb.name,)),
                                 V(aT.t[:, fc * TT:fc * TT + T], (aT.name,)), start=(fc == 0), stop=(fc == FCN - 1))
                    if f0 + nf == FCN:
                        xt = state["xt"]
                        for j in range(4):
                            dd = dh * 4 + j
                            P.stt(xt[:, dd, :T], acc[j][:, :T], mdv(jhg, col, dd), xt[:, dd, :T], ALU.mult, ALU.add)
                        if dh == 1:
                            P.dma("pool", xs_v(t0, T), xt[:, :, :T], xring.ssem[xt.name])
                return (ld, cp)

            for fb2 in range(11):
                jobs.append(stage1(fb2, first=(fb2 == 0)))
            for dh in range(2):
                for fg in range(3):
                    jobs.append(stage2(dh, fg))
        run_jobs(jobs, 3)

    def mixer_pass(l, d, tlist):
        new_phase()
        win, wout = wb["win"], wb["wout"]
        qs = AFa.carve("qs", 4 * TT)
        fr4 = [AFa.carve(f"fr{i}", TT) for i in range(4)]
        lf = AFa.carve("lf", TT)
        kk = AFa.carve("kk", TT)
        bb = AFa.carve("bb", TT)
        a16 = AFa.carve("a16", TT)
        Ab = AFa.carve("Ab", TT)
        aB = AFa.carve("aB", TT)
        ee = AFa.carve("ee", TT)
        tD = [AFa.carve(f"tD{i}", TT) for i in range(2)]
        x1 = AFa.carve("x1", TT)
        dn2 = [AFa.carve(f"dn{i}", 8) for i in range(2)]
        S_all = AFa.carve("S_all", 9 * 128)
        obuf = AFa.carve("obuf", 4 * TT)
        if d == 0:
            gv = AFa.carve("gv", 512)
            junk = AFa.carve("junk", 128)
            ssv = AFa.carve("ssv", 16)
            tg = AFa.carve("tg", TT)
        v_tm = ABa.carve("v_tm", 4 * 512)
        qt2 = [ABa.carve(f"qt{j}", TT) for j in range(2)]
        qb2 = [ABa.carve(f"qb{j}", TT) for j in range(2)]
        kt2 = [[ABa.carve(f"kt{j}{i}", TT) for i in range(4)] for j in range(2)]
        for j in range(2):
            for i in range(4):
                P.memset("pool", kt2[j][i][:, :], 0.0)
        ke2 = [ABa.carve(f"ke{j}", TT) for j in range(2)]
        ke_tm = [ABa.carve(f"ke_tm{i}", TT) for i in range(2)]
        for i in range(2):
            P.memset("dve", ke_tm[i][:, :], 0.0)
        Sb = ABa.carve("Sb", 8 * 128)
        PT = ABa.carve("PT", TT)
        if d == 0:
            vn_tm = ABa.carve("vn_tm", 4 * 512)
            mixT = ABa.carve("mixT", 8 * TT)
            osq = ABa.carve("osq", TT)
            gu = ABa.carve("gu", 4 * TT)
            sog = ABa.carve("sog", 4 * TT)
        m64 = V(cst.t[:, C_M64:C_M64 + 512], cst[:].keys)
        m16 = V(cst.t[:, C_M16:C_M16 + 512], cst[:].keys)
        cmask = V(cst.t[:, (C_MF if d == 0 else C_MB):(C_MF if d == 0 else C_MB) + 512], cst[:].keys)
        state = {}

        def c3(buf, T, s=64):
            return V(buf.t[:, :T].rearrange("p (n s) -> p n s", s=s), (buf.name,))

        def scan_E(hd, T, frb, st):
            NCH, NP = T // 64, T // 128
            qt, qb, kt, ke, dn = qt2[st], qb2[st], kt2[st], ke2[st], dn2[st]
            qsh = V(qs.t[:, hd * TT:hd * TT + T], (qs.name,))
            fr = fr4[hd]
            P.ts("dve", fr[:, :T], fr[:, :T], V(omlt[d].t[:, hd, l:l + 1], omlt[d][:].keys), ALU.mult,
                 V(lbt[d].t[:, hd, l:l + 1], lbt[d][:].keys), ALU.add)
            P.act(lf[:, :T], fr[:, :T], AF.Ln)
            P.ts("dve", kk[:, :T], fr[:, :T], -1.0, ALU.mult, 1.0, ALU.add)
            P.scan(bb[:, :T], V(m64.ap[:, :T], m64.keys), lf[:, :T], 0.0, ALU.mult, ALU.add)
            P.scan(a16[:, :T], V(m16.ap[:, :T], m16.keys), lf[:, :T], 0.0, ALU.mult, ALU.add)
            bend = V(bc(bb.t[:, 63:64], [[64, NCH], [0, 64]]), (bb.name,))
            if d == 0:
                A, a = bb, a16
            else:
                P.tt("dve", c3(Ab, T), bend, c3(bb, T), ALU.subtract)
                P.tt("dve", Ab[:, :T], Ab[:, :T], lf[:, :T], ALU.add)
                tot = V(bc(a16.t[:, 15:16], [[16, T // 16], [0, 16]]), (a16.name,))
                P.tt("dve", c3(aB, T, 16), tot, c3(a16, T, 16), ALU.subtract)
                P.tt("dve", aB[:, :T], aB[:, :T], lf[:, :T], ALU.add)
                A, a = Ab, aB
            def c3s(buf, r0, r1):
                return V(buf.t[:, :T].rearrange("p (n s) -> p n s", s=64)[:, :, r0:r1], (buf.name,))

            P.act(a[:, :T], a[:, :T], AF.Exp)
            P.act(ee[:, :T], A[:, :T], AF.Exp)
            P.tt("pool", qt[:, :T], qsh, a[:, :T], ALU.mult)
            P.tt("pool", qb[:, :T], qsh, ee[:, :T], ALU.mult)
            for i in range(4):
                r0, r1 = (0, 16 * (i + 1)) if d == 0 else (16 * i, 64)
                tDj = tD[i % 2]
                if (d == 0 and i == 0) or (d == 1 and i == 3):
                    P.act(c3s(tDj, r0, r1), c3s(A, r0, r1), AF.Exp, scale=-1.0)
                else:
                    colr = 16 * i - 1 if d == 0 else 16 * (i + 1)
                    refb = V(bc(A.t[:, colr:colr + 1], [[64, NCH], [0, r1 - r0]]), (A.name,))
                    P.tt("dve", c3s(tDj, r0, r1), c3s(A, r0, r1), refb, ALU.subtract)
                    P.act(c3s(tDj, r0, r1), c3s(tDj, r0, r1), AF.Exp, scale=-1.0)
                P.tt("pool", c3s(kt[i], r0, r1), c3s(kk, r0, r1), c3s(tDj, r0, r1), ALU.mult)
            if d == 0:
                P.tt("dve", c3(x1, T), bend, c3(bb, T), ALU.subtract)
            else:
                P.tt("dve", x1[:, :T], bb[:, :T], lf[:, :T], ALU.subtract)
            P.act(x1[:, :T], x1[:, :T], AF.Exp)
            P.tt("pool", ke[:, :T], kk[:, :T], x1[:, :T], ALU.mult)
            P.act(dn[:, :NCH], V(bb.t[:, 63:T:64], (bb.name,)), AF.Exp)

        def scan_M(hd, T, st):
            NCH, NP = T // 64, T // 128
            order = list(range(NCH)) if d == 0 else list(range(NCH - 1, -1, -1))
            qt, qb, kt, ke, dn = qt2[st], qb2[st], kt2[st], ke2[st], dn2[st]
            for p in range(NP):
                P.transpose(psT[:, p * 128:(p + 1) * 128], ke[:, p * 128:(p + 1) * 128], identb[:])
            P.copy("act", V(ke_tm[0].t[0:64, :T], (ke_tm[0].name,)), V(psT.t[0:64, :T], (psT.name,)))
            P.copy("act", V(ke_tm[1].t[64:128, :T], (ke_tm[1].name,)), V(psT.t[64:128, :T], (psT.name,)))
            ub = [bank() for _ in range((NCH + 3) // 4)]
            for n in range(NCH):
                p, hf = n // 2, n % 2
                P.mm(ub[n // 4][:, (n % 4) * 128:(n % 4 + 1) * 128],
                     V(ke_tm[hf].t[:, p * 128:(p + 1) * 128], (ke_tm[hf].name,)),
                     V(v_tm.t[:, p * 512 + hd * 128:p * 512 + (hd + 1) * 128], (v_tm.name,)))
            P.copy("dve", S_all[:, 0:128], Sst[d][hd][:])
            for j, n in enumerate(order):
                P.stt(S_all[:, (j + 1) * 128:(j + 2) * 128], S_all[:, j * 128:(j + 1) * 128], dn[:, n:n + 1],
                      ub[n // 4][:, (n % 4) * 128:(n % 4 + 1) * 128], ALU.mult, ALU.add)
            P.copy("dve", Sst[d][hd][:], S_all[:, NCH * 128:(NCH + 1) * 128])
            P.copy("act", Sb[:, :NCH * 128], S_all[:, :NCH * 128])
            for n in range(NCH):
                p, hf = n // 2, n % 2
                for i in range(4):
                    c0 = p * 128 + hf * 64 + 16 * i
                    P.mm(V(psS.t[hf * 64:(hf + 1) * 64, c0:c0 + 16], (psS.name,)),
                         kt[i][:, n * 64:(n + 1) * 64], qt[:, n * 64 + 16 * i:n * 64 + 16 * i + 16])
            P.tt("dve", PT[:, :T], psS[:, :T], V(cmask.ap[:, :T], cmask.keys), ALU.mult)
            po = bank()
            first = True
            for p in range(NP):
                P.mm(po[:, p * 128:(p + 1) * 128],
                     V(v_tm.t[:, p * 512 + hd * 128:p * 512 + (hd + 1) * 128], (v_tm.name,)),
                     PT[:, p * 128:(p + 1) * 128], start=first, stop=False, skip_group_check=True)
                first = False
                for n in (2 * p, 2 * p + 1):
                    j = order.index(n)
                    P.mm(po[:, n * 64:(n + 1) * 64], Sb[:, j * 128:(j + 1) * 128], qb[:, n * 64:(n + 1) * 64],
                         start=False, stop=(n == NCH - 1), skip_group_check=True)
            oh = V(obuf.t[:, hd * TT:hd * TT + T], (obuf.name,))
            if d == 1:
                P.copy("act", oh, po[:, :T])
            else:
                P.tt("dve", oh, po[:, :T], oh, ALU.add)

        def sec_load(c0):
            def view(t):
                return t[:, :].rearrange("p (k n) -> p k n", n=512)
            return slab_load(win.t[l, :, c0:c0 + 512].rearrange("(k p) n -> p k n", p=128), view, l)

        jobs = []
        for ti, (t0, T, col) in enumerate(tlist):
            NP = T // 128

            def proj_fm(sb, T, evac):
                for hh in range(4):
                    pb = bank()
                    for kc in range(DC):
                        o = kc * 512 + hh * 128
                        P.mm(pb[:, :T], V(sb.t[:, o:o + 128], (sb.name,)), hT[:, kc, :T], start=(kc == 0), stop=(kc == DC - 1))
                    evac(hh, pb)

            def proj_tm(sb, T, evac):
                for p in range(T // 128):
                    pb = bank()
                    for kc in range(DC):
                        P.mm(pb[:, :], hT[:, kc, p * 128:(p + 1) * 128], V(sb.t[:, kc * 512:(kc + 1) * 512], (sb.name,)),
                             start=(kc == 0), stop=(kc == DC - 1))
                    evac(p, pb)

            def job_q(t0=t0, T=T, col=col):
                sl = sec_load(1024)

                def ld():
                    xt, sx = xring.next()
                    P.dma("sp", xt[:, :, :T], xs_v(t0, T), sx)
                    if d == 0:
                        P.dma("sp", V(obuf.t[:, :].rearrange("p (h t) -> p h t", t=TT)[:, :, :T], (obuf.name,)), ob_v(t0, T), s_obl)
                    return (xt, sl())

                def cp(hd_):
                    xt, sb = hd_
                    state["xt"] = xt
                    norm_h(xt, T, 4, 3, col)
                    proj_fm(sb, T, lambda hh, pb: P.act(V(qs.t[:, hh * TT:hh * TT + T], (qs.name,)), pb[:, :T], AF.Silu))
                return (ld, cp)

            def job_i(T=T):
                def cp(sb):
                    proj_tm(sb, T, lambda p, pb: P.copy("act", V(v_tm.t[:, p * 512:(p + 1) * 512], (v_tm.name,)), pb[:, :]))
                return (sec_load(2560), cp)

            def job_f(t0=t0, T=T):
                def cp(sb):
                    proj_fm(sb, T, lambda hh, pb: P.act(fr4[hh][:, :T], pb[:, :T], AF.Sigmoid))

                    def stage_E(hh):
                        scan_E(hh, T, None, hh % 2)
                    stage_E(0)
                    for hh in range(4):
                        if hh + 1 < 4:
                            stage_E(hh + 1)
                        scan_M(hh, T, hh % 2)
                    if d == 1:
                        P.dma("pool", ob_v(t0, T), V(obuf.t[:, :].rearrange("p (h t) -> p h t", t=TT)[:, :, :T], (obuf.name,)), s_ob)
                return (sec_load(1536 if d == 0 else 2048), cp)

            def job_u(T=T):
                def cp(sb):
                    proj_fm(sb, T, lambda hh, pb: P.act(V(gu.t[:, hh * TT:hh * TT + T], (gu.name,)), pb[:, :T], AF.Gelu_apprx_tanh))
                return (sec_load(0), cp)

            def job_og(T=T):
                def cp(sb):
                    proj_fm(sb, T, lambda hh, pb: P.act(V(sog.t[:, hh * TT:hh * TT + T], (sog.name,)), pb[:, :T], AF.Silu))
                return (sec_load(3072), cp)

            def job_v(T=T):
                def cp(sb):
                    gvb = V(bpar.t[:, 0:512], bpar[:].keys)

                    def ev(p, pb, gv0=gv):
                        gv = (gv0, x1)[p % 2]
                        P.act(gv[:, :], pb[:, :], AF.Gelu_apprx_tanh)
                        for g in range(4):
                            P.act(junk[:, :], gv[:, g * 128:(g + 1) * 128], AF.Square, accum_out=ssv[:, p * 4 + g:p * 4 + g + 1])
                        P.act(ssv[:, p * 4:p * 4 + 4], ssv[:, p * 4:p * 4 + 4], AF.Ln, bias=eps_t[:, 0:1], scale=1.0 / 128)
                        P.act(ssv[:, p * 4:p * 4 + 4], ssv[:, p * 4:p * 4 + 4], AF.Exp, scale=-0.5)
                        for g in range(4):
                            P.stt(V(vn_tm.t[:, p * 512 + g * 128:p * 512 + (g + 1) * 128], (vn_tm.name,)), gv[:, g * 128:(g + 1) * 128],
                                  ssv[:, p * 4 + g:p * 4 + g + 1], V(gvb.ap[:, g * 128:(g + 1) * 128], gvb.keys), ALU.mult, ALU.mult)
                    proj_tm(sb, T, ev)
                    for g in range(4):
                        pb = bank()
                        for p in range(T // 128):
                            P.mm(pb[:, p * 128:(p + 1) * 128], V(vn_tm.t[:, p * 512 + g * 128:p * 512 + (g + 1) * 128], (vn_tm.name,)),
                                 V(wsb.t[:, l * 512 + g * 128:l * 512 + (g + 1) * 128], wsb[:].keys))
                        P.tt("dve", tg[:, :T], pb[:, :T], V(bpar.t[:, 512 + g * 512:512 + g * 512 + T], bpar[:].keys), ALU.add)
                        P.tt("dve", V(mixT.t[:, g * TT:g * TT + T], (mixT.name,)), tg[:, :T], V(gu.t[:, g * TT:g * TT + T], (gu.name,)), ALU.mult)
                return (sec_load(512), cp)

            def job_comb(T=T):
                def cp(_):
                    sqs = [qt2[0], qt2[1], qb2[0], qb2[1]]
                    rrs = [lf, kk, bb, a16]
                    tgs = [Ab, aB, ee, x1]
                    for hd in range(4):
                        oh = V(obuf.t[:, hd * TT:hd * TT + T], (obuf.name,))
                        P.act(sqs[hd][:, :T], oh, AF.Square)
                        pb = bank()
                        P.mm(pb[:, :T], onesb[:], sqs[hd][:, :T])
                        P.act(rrs[hd][:, :T], pb[:, :T], AF.Ln, bias=eps_t[:, 0:1], scale=1.0 / 128)
                        P.act(rrs[hd][:, :T], rrs[hd][:, :T], AF.Exp, scale=-0.5)
                        P.tt("dve", tgs[hd][:, :T], oh, rrs[hd][:, :T], ALU.mult)
                        P.stt(V(mixT.t[:, (4 + hd) * TT:(4 + hd) * TT + T], (mixT.name,)), tgs[hd][:, :T], sp_col("go", l * 4 + hd),
                              V(sog.t[:, hd * TT:hd * TT + T], (sog.name,)), ALU.mult, ALU.mult)
                return (None, cp)

            def job_out(half, t0=t0, T=T, col=col):
                def view(t):
                    return t[:, :].rearrange("p (k n) -> p k n", n=512)

                def cp(sb):
                    xt = state["xt"]
                    for j in range(4):
                        dd = half * 4 + j
                        pb = bank()
                        for kc in range(DC):
                            o = kc * 512 + j * 128
                            P.mm(pb[:, :T], V(sb.t[:, o:o + 128], (sb.name,)), V(mixT.t[:, kc * TT:kc * TT + T], (mixT.name,)),
                                 start=(kc == 0), stop=(kc == DC - 1))
                        P.stt(xt[:, dd, :T], pb[:, :T], mdv(5, col, dd), xt[:, dd, :T], ALU.mult, ALU.add)
                    if half == 1:
                        P.dma("pool", xs_v(t0, T), xt[:, :, :T], xring.ssem[xt.name])
                return (slab_load(wout.t[l, :, half * 512:(half + 1) * 512].rearrange("(k p) n -> p k n", p=128), view, l), cp)

            jobs.append(job_q())
            jobs.append(job_i())
            if d == 0:
                jobs.append(job_u())
                jobs.append(job_og())
                jobs.append(job_v())
            jobs.append(job_f())
            if d == 0:
                jobs.append(job_comb())
                jobs.append(job_out(0))
                jobs.append(job_out(1))
        run_jobs(jobs, 3)

    def final_norm():
        new_phase()
        ot = AFa.carve("ot", DC * TT)
        for (t0, T, col) in tiles[:NT]:
            xt, sx = xring.next()
            P.dma("sp", xt[:, :, :T], xs_v(t0, T), sx)
            pb = bank()
            for kc in range(DC):
                sq_ = V(sqb.t[:, kc % 2, :T], (("sqb", kc % 2),))
                P.act(sq_, xt[:, kc, :T], AF.Square)
                P.mm(pb[:, :T], onesb[:], sq_, start=(kc == 0), stop=(kc == DC - 1))
            P.act(rr[:, :T], pb[:, :T], AF.Ln, bias=eps_t[:, 0:1], scale=1.0 / D)
            P.act(rstd[:, :T], rr[:, :T], AF.Exp, scale=-0.5)
            for kc in range(DC):
                P.stt(V(ot.t[:, kc * TT:kc * TT + T], (ot.name,)), xt[:, kc, :T], sp_col("gfin", kc), rstd[:, :T], ALU.mult, ALU.mult)
            P.dma("pool", V(outT.t[:, :, t0:t0 + T], ()), V(ot.t[:, :].rearrange("p (k t) -> p k t", t=TT)[:, :, :T], (ot.name,)), s_ot)

    lat_tiles = tiles[:NT]
    ctx_tile = tiles[NT:]
    for l in range(depth):
        modulation(l)
        if l + 1 < depth:
            convert(l + 1, 0)
            convert(l + 1, 1)
        if stop_after == "mod":
            break
        ffn_pass(l, 1, tiles)
        if stop_after == "ffn1":
            break
        for dd in range(2):
            for hd in range(4):
                P.memset("dve", Sst[dd][hd][:], 0.0)
        mixer_pass(l, 1, ctx_tile)
        if stop_after == "mB":
            break
        mixer_pass(l, 0, ctx_tile)
        if stop_after == "mF":
            break
        mixer_pass(l, 1, lat_tiles[::-1])
        mixer_pass(l, 0, lat_tiles)
        if stop_after == "mix":
            break
        ffn_pass(l, 2, tiles if l < depth - 1 else lat_tiles)
    if stop_after is None:
        final_norm()
    P.barrier()
    P.close()
    return nc, P


def prep_inputs(inp, L, depth, nb):
    f32 = lambda a: np.ascontiguousarray(np.asarray(a, dtype=np.float32))
    soff, NSP = spar_layout(depth)
    spar = np.zeros((128, NSP), np.float32)
    g = np.stack([f32(inp["norm_ffn1_g"])[:depth], f32(inp["norm_mix_g"])[:depth], f32(inp["norm_ffn2_g"])[:depth]], axis=1)
    spar[:, soff["g"]:soff["g"] + depth * 24] = fmv(g).reshape(128, depth * 24)
    spar[:, soff["gfin"]:soff["gfin"] + 8] = fmv(f32(inp["norm_final_g"]))
    spar[:, soff["go"]:soff["go"] + depth * 4] = fmv(f32(inp["hgrn_norm_g"])[:depth]).reshape(128, depth * 4)
    for nm, key in (("lbf", "hgrn_lb_fwd"), ("lbb", "hgrn_lb_bwd")):
        a = fmv(f32(inp[key])[:depth])
        spar[:, soff[nm]:soff[nm] + 4 * depth] = a.transpose(0, 2, 1).reshape(128, 4 * depth)
    spar[:, soff["bada"]:soff["bada"] + depth * 72] = fmv(f32(inp["b_ada"])[:depth]).reshape(128, depth * 72)
    bpar = np.zeros((depth, 128, 512 + 2048), np.float32)
    gng = f32(inp["gmlp_norm_g"])
    gbs = f32(inp["gmlp_bs"])
    for l in range(depth):
        bpar[l, :, 0:512] = gng[l][None, :]
        bpar[l, :, 512:] = np.tile(gbs[l][:, None, :], (1, 4, 1)).reshape(1, 2048)
    ws = f32(inp["gmlp_ws"])[:depth]
    wsT = np.ascontiguousarray(ws.transpose(3, 0, 1, 2)).reshape(128, depth * 512)
    cst = make_consts()
    pos = pos_table(L)
    shared = {
        "posT": pos, "spar": spar, "bpar": bpar, "wsT": wsT, "cst": cst,
        "w_ada": f32(inp["w_ada"])[:depth],
        "f1w1": f32(inp["ffn1_w1"])[:depth], "f1w3": f32(inp["ffn1_w3"])[:depth], "f1w2": f32(inp["ffn1_w2"])[:depth],
        "f2w1": f32(inp["ffn2_w1"])[:depth], "f2w3": f32(inp["ffn2_w3"])[:depth], "f2w2": f32(inp["ffn2_w2"])[:depth],
        "win": f32(inp["w_in"])[:depth], "wout": f32(inp["w_out"])[:depth],
    }
    x = f32(inp["x"])
    ctx = f32(inp["ctx"])
    c = f32(inp["c"])
    cc = f32(inp["c_ctx"])
    maps = []
    for b in range(nb):
        m = dict(shared)
        m["xT"] = fm(x[b])
        m["ctxT"] = fm(ctx[b])
        m["cv"] = np.ascontiguousarray(np.stack([fmv(c[b]), fmv(cc)], axis=-1))
        maps.append(m)
    return maps


def run(inp, L, depth, nb, trace=False):
    nc, P = build(L, depth)
    maps = prep_inputs(inp, L, depth, nb)
    res = run_bass_kernel_spmd(nc, maps, core_ids=list(range(nb)))
    outs = []
    for b in range(nb):
        o = res.results[b]["outT"]
        outs.append(np.ascontiguousarray(o.transpose(2, 1, 0).reshape(L, D)))
    return np.stack(outs, axis=0)


def kernel(**inputs):
    x = np.asarray(inputs["x"])
    B, L, _ = x.shape
    depth = np.asarray(inputs["w_ada"]).shape[0]
    return run(inputs, L, depth, B).astype(np.float32)
```

```python
import numpy as np
import concourse.bass as bass
import concourse.mybir as mybir
from concourse.ap import AP
from concourse.bass_utils import run_bass_kernel_spmd

F32 = mybir.dt.float32
BF16 = mybir.dt.bfloat16
AF = mybir.ActivationFunctionType
ALU = mybir.AluOpType
AX = mybir.AxisListType

D = 1024
DC = 8
DFF = 2816
FCN = 22
INW = 3584
NCTX = 256
GRID_W = 64
EPS = 1e-6
CAP = 80.0
TT = 512


class V:
    __slots__ = ("ap", "keys")

    def __init__(self, ap, keys):
        self.ap = ap
        self.keys = keys


class Buf:
    def __init__(self, t, name):
        self.t = t
        self.name = name

    def __getitem__(self, idx):
        return V(self.t[idx], (self.name,))

    def v(self, ap):
        return V(ap, (self.name,))


def vkeys(x):
    return x.keys if isinstance(x, V) else ()


def vap(x):
    return x.ap if isinstance(x, V) else x


def bc(ap, pattern):
    return AP(ap.tensor, ap.offset, [list(ap.ap[0])] + [list(p) for p in pattern])


class Prog:
    ENG = ("pe", "act", "dve", "pool", "sp")

    def __init__(self, nc):
        self.nc = nc
        self.e = {"pe": nc.tensor, "act": nc.scalar, "dve": nc.vector, "pool": nc.gpsimd, "sp": nc.sync}
        self.stack = []
        self.sems = {}
        self.cnt = {}
        self.seen = {e: {} for e in self.ENG}
        self.last_w = {}
        self.readers = {}
        self.ninst = 0
        self.nbank = 0
        self.epoch = 0
        self.esem = {}
        for e in self.ENG:
            self.esem[e] = self.new_sem("E_" + e)

    def new_sem(self, name):
        cm = self.nc.semaphore(name)
        s = cm.__enter__()
        self.stack.append(cm)
        self.sems[name] = s
        self.cnt[name] = 0
        return name

    def sbuf(self, name, shape, dtype):
        cm = self.nc.sbuf_tensor(name, list(shape), dtype)
        t = cm.__enter__()
        self.stack.append(cm)
        return Buf(t, name)

    def psum(self, name, shape, dtype=F32):
        cm = self.nc.psum_tensor(name, list(shape), dtype)
        t = cm.__enter__()
        self.stack.append(cm)
        return Buf(t, name)

    def dram(self, name, shape, dtype, kind="Internal"):
        t = self.nc.dram_tensor(name, list(shape), dtype, kind=kind)
        return Buf(t.ap(), name)

    def close(self):
        for cm in reversed(self.stack):
            cm.__exit__(None, None, None)
        self.stack = []

    def _need(self, eng, tok):
        kind, sem = tok[0], tok[1]
        val = tok[2]
        if kind != "dma" and sem == self.esem[eng] and eng == "pe":
            return
        if self.seen[eng].get(sem, 0) >= val:
            return
        self.e[eng].wait_ge(self.sems[sem], val)
        self.seen[eng][sem] = val

    def _deps(self, eng, reads, writes):
        for k in reads:
            t = self.last_w.get(k)
            if t is not None:
                self._need(eng, t)
        for k in writes:
            t = self.last_w.get(k)
            if t is not None:
                self._need(eng, t)
            for t in self.readers.get(k, ()):
                self._need(eng, t)

    def _record(self, tok, reads, writes):
        for k in writes:
            self.last_w[k] = tok
            self.readers[k] = []
        for k in reads:
            if k in writes:
                continue
            lst = self.readers.setdefault(k, [])
            lst[:] = [t for t in lst if t[1] != tok[1]]
            lst.append(tok)

    def op(self, eng, fn, outs, ins):
        writes = [k for o in outs for k in vkeys(o)]
        reads = [k for i in ins for k in vkeys(i)]
        if self.cnt[self.esem[eng]] >= 30000:
            self.barrier()
            self.epoch += 1
            self.esem[eng] = self.new_sem("E_%s_%d" % (eng, self.epoch))
        self._deps(eng, reads, writes)
        inst = fn()
        sem = self.esem[eng]
        self.cnt[sem] += 1
        inst.then_inc(self.sems[sem], 1)
        self._record(("eng", sem, self.cnt[sem]), reads, writes)
        self.ninst += 1
        return inst

    def dma(self, q, out, in_, sem, **kw):
        writes = list(vkeys(out))
        reads = list(vkeys(in_))
        self._deps(q, reads, writes)
        inst = self.e[q].dma_start(out=vap(out), in_=vap(in_), **kw)
        inst.then_inc(self.sems[sem], 16)
        self.cnt[sem] += 1
        self._record(("dma", sem, 16 * self.cnt[sem]), reads, writes)
        self.ninst += 1
        return inst

    def _all_tokens(self, skip_prefix=None):
        toks = set()
        for name, c in self.cnt.items():
            if c == 0:
                continue
            if skip_prefix and name.startswith(skip_prefix):
                continue
            if name.startswith("E_"):
                toks.add(("eng", name, c))
            else:
                toks.add(("dma", name, 16 * c))
        return toks

    def wait_all(self, eng, skip_prefix=None):
        for t in self._all_tokens(skip_prefix):
            self._need(eng, t)

    def barrier(self, skip_prefix=None):
        for e in self.ENG:
            self.wait_all(e, skip_prefix)

    def mm(self, out, lhsT, rhs, start=True, stop=True, **kw):
        return self.op("pe", lambda: self.nc.tensor.matmul(vap(out), vap(lhsT), vap(rhs), start=start, stop=stop, **kw),
                       [out], [lhsT, rhs])

    def transpose(self, out, in_, ident):
        return self.op("pe", lambda: self.nc.tensor.transpose(vap(out), vap(in_), vap(ident)), [out], [in_, ident])

    def act(self, out, in_, func, bias=0.0, scale=1.0, accum_out=None):
        outs = [out] + ([accum_out] if accum_out is not None else [])
        kw = {}
        if accum_out is not None:
            kw["accum_out"] = vap(accum_out)
        return self.op("act", lambda: self.nc.scalar.activation(out=vap(out), in_=vap(in_), func=func, bias=vap(bias),
                                                                 scale=vap(scale), **kw),
                       outs, [in_, bias, scale])

    def tt(self, eng, out, in0, in1, op):
        return self.op(eng, lambda: self.e[eng].tensor_tensor(out=vap(out), in0=vap(in0), in1=vap(in1), op=op),
                       [out], [in0, in1])

    def ts(self, eng, out, in0, s1, op0, s2=None, op1=None):
        def fn():
            if op1 is None:
                return self.e[eng].tensor_scalar(out=vap(out), in0=vap(in0), scalar1=vap(s1), scalar2=None, op0=op0)
            return self.e[eng].tensor_scalar(out=vap(out), in0=vap(in0), scalar1=vap(s1), scalar2=vap(s2), op0=op0, op1=op1)
        return self.op(eng, fn, [out], [in0, s1, s2])

    def stt(self, out, in0, scalar, in1, op0, op1):
        return self.op("dve", lambda: self.nc.vector.scalar_tensor_tensor(out=vap(out), in0=vap(in0), scalar=vap(scalar),
                                                                           in1=vap(in1), op0=op0, op1=op1),
                       [out], [in0, scalar, in1])

    def scan(self, out, d0, d1, initial, op0, op1):
        return self.op("dve", lambda: self.nc.vector.tensor_tensor_scan(out=vap(out), data0=vap(d0), data1=vap(d1),
                                                                         initial=vap(initial), op0=op0, op1=op1),
                       [out], [d0, d1, initial])

    def copy(self, eng, out, in_):
        if eng == "act":
            return self.op("act", lambda: self.nc.scalar.copy(out=vap(out), in_=vap(in_)), [out], [in_])
        return self.op(eng, lambda: self.e[eng].tensor_copy(out=vap(out), in_=vap(in_)), [out], [in_])

    def recip(self, out, in_):
        return self.op("dve", lambda: self.nc.vector.reciprocal(out=vap(out), in_=vap(in_)), [out], [in_])

    def reduce(self, out, in_, op):
        return self.op("dve", lambda: self.nc.vector.tensor_reduce(out=vap(out), in_=vap(in_), axis=AX.X, op=op),
                       [out], [in_])

    def memset(self, eng, out, val):
        return self.op(eng, lambda: self.e[eng].memset(vap(out), val), [out], [])


class Ring:
    def __init__(self, P, name, n, shape, dtype):
        self.bufs = [P.sbuf(f"{name}{i}", shape, dtype) for i in range(n)]
        self.sems = [P.new_sem(f"S_{name}{i}") for i in range(n)]
        self.ssem = {b.name: P.new_sem(f"T_{name}{i}") for i, b in enumerate(self.bufs)}
        self.n = n
        self.i = 0

    def next(self):
        j = self.i % self.n
        self.i += 1
        return self.bufs[j], self.sems[j]


def run_jobs(jobs, depth):
    handles = {}
    n = len(jobs)
    for j in range(min(depth - 1, n)):
        handles[j] = jobs[j][0]() if jobs[j][0] else None
    for i in range(n):
        j = i + depth - 1
        if j < n:
            handles[j] = jobs[j][0]() if jobs[j][0] else None
        jobs[i][1](handles.pop(i))


def spar_layout(depth):
    off = {}
    o = 0
    for name, n in (("g", depth * 24), ("gfin", 8), ("go", depth * 4), ("lbf", 4 * depth), ("lbb", 4 * depth),
                    ("bada", depth * 72)):
        off[name] = o
        o += n
    return off, o


C_M64, C_M16, C_MF, C_MB, C_ID, C_ONE, C_N = 0, 512, 1024, 1536, 2048, 2176, 2304


def make_consts():
    c = np.zeros((128, C_N), np.float32)
    t = np.arange(512)
    c[:, C_M64:C_M64 + 512] = (t % 64 != 0).astype(np.float32)[None]
    c[:, C_M16:C_M16 + 512] = (t % 16 != 0).astype(np.float32)[None]
    s = np.arange(128)[:, None]
    tt_ = np.arange(128)[None, :]
    same = (s // 64) == (tt_ // 64)
    mf = (same & (s <= tt_)).astype(np.float32)
    mb = (same & (s >= tt_)).astype(np.float32)
    c[:, C_MF:C_MF + 512] = np.tile(mf, (1, 4))
    c[:, C_MB:C_MB + 512] = np.tile(mb, (1, 4))
    c[:, C_ID:C_ID + 128] = np.eye(128, dtype=np.float32)
    c[:, C_ONE:C_ONE + 128] = 1.0
    return c


def pos_table(L):
    rows = L // GRID_W
    row = np.repeat(np.arange(rows), GRID_W).astype(np.float32)
    col = np.tile(np.arange(GRID_W), rows).astype(np.float32)
    quarter = D // 4
    freq = (np.float32(10000.0) ** (-np.arange(quarter, dtype=np.float32) / np.float32(quarter))).astype(np.float32)

    def axis_embed(p):
        a = (p[:, None] * freq[None, :]).astype(np.float32)
        return np.concatenate([np.sin(a), np.cos(a)], axis=-1)

    pe = np.concatenate([axis_embed(row), axis_embed(col)], axis=-1).astype(np.float32)
    return fm(pe)


def fm(a):
    n = a.shape[0]
    return np.ascontiguousarray(a.T.reshape(DC, 128, n).transpose(1, 0, 2))


def fmv(v):
    k = v.shape[-1] // 128
    r = v.reshape(v.shape[:-1] + (k, 128))
    return np.moveaxis(r, -1, 0)


def build(L, depth, stop_after=None):
    nc = bass.Bass("TRN2", target_bir_lowering=False)
    P = Prog(nc)
    NT = L // TT
    NTOK = L + NCTX
    soff, NSP = spar_layout(depth)
    NBP = 512 + 2048

    xT = P.dram("xT", [128, DC, L], F32, "ExternalInput")
    posT = P.dram("posT", [128, DC, L], F32, "ExternalInput")
    ctxT = P.dram("ctxT", [128, DC, NCTX], F32, "ExternalInput")
    cvd = P.dram("cv", [128, DC, 2], F32, "ExternalInput")
    spard = P.dram("spar", [128, NSP], F32, "ExternalInput")
    bpard = P.dram("bpar", [depth, 128, NBP], F32, "ExternalInput")
    wsTd = P.dram("wsT", [128, depth * 512], F32, "ExternalInput")
    cstd = P.dram("cst", [128, C_N], F32, "ExternalInput")
    w_ada = P.dram("w_ada", [depth, D, 9 * D], F32, "ExternalInput")
    wnames = [("f1w1", D, DFF), ("f1w3", D, DFF), ("f1w2", DFF, D), ("f2w1", D, DFF), ("f2w3", D, DFF),
              ("f2w2", DFF, D), ("win", D, INW), ("wout", D, D)]
    wf = {n: P.dram(n, [depth, r, c], F32, "ExternalInput") for n, r, c in wnames}
    wb = {n: P.dram(n + "_b", [depth, r, c], BF16) for n, r, c in wnames}
    outT = P.dram("outT", [128, DC, L], F32, "ExternalOutput")
    xs = P.dram("xs", [128, DC, NTOK], F32)
    obd = P.dram("obd", [128, 4, NTOK], F32)

    def xs_v(t0, T):
        return V(xs.t[:, :, t0:t0 + T], (("xs", t0),))

    def ob_v(t0, T):
        return V(obd.t[:, :, t0:t0 + T], (("ob", t0),))

    spar = P.sbuf("spar_t", [128, NSP], F32)
    bpar = P.sbuf("bpar_t", [128, NBP], F32)
    wsb = P.sbuf("wsb", [128, depth * 512], BF16)
    cst = P.sbuf("cst_t", [128, C_N], F32)
    identb = P.sbuf("identb", [128, 128], BF16)
    onesb = P.sbuf("onesb", [128, 128], BF16)
    cv = P.sbuf("cv_t", [128, DC, 2], F32)
    scv = P.sbuf("scv", [128, DC, 2], F32)
    mod = P.sbuf("mod", [128, 72, 2], F32)
    md = P.sbuf("md", [128, 9, DC, 2], F32)
    lbt = [P.sbuf(f"lbt{d}", [128, 4, depth], F32) for d in range(2)]
    omlt = [P.sbuf(f"omlt{d}", [128, 4, depth], F32) for d in range(2)]
    lbtmp = P.sbuf("lbtmp", [128, 4, depth], F32)
    lbm = P.sbuf("lbm", [128, 4], F32)
    Sst = [[P.sbuf(f"Sst{d}{h}", [128, 128], F32) for h in range(4)] for d in range(2)]
    slabs = Ring(P, "slab", 3, [128, 4096], BF16)
    xring = Ring(P, "xt", 2, [128, DC, TT], F32)
    hT = P.sbuf("hT", [128, DC, TT], BF16)
    sqb = P.sbuf("sqb", [128, 2, TT], BF16)
    tmpn = P.sbuf("tmpn", [128, TT], F32)
    tmpn_b = P.sbuf("tmpn_b", [128, TT], F32)
    tmpn2 = [tmpn, tmpn_b]
    rr = P.sbuf("rr", [128, TT], F32)
    rstd = P.sbuf("rstd", [128, TT], F32)
    arF = P.sbuf("arF", [128, 13824], F32)
    arB = P.sbuf("arB", [128, 22784], BF16)
    banks = [P.psum(f"pb{i}", [128, 512], F32) for i in range(6)]
    psS = P.psum("psS", [128, 512], F32)
    psT = P.psum("psT", [128, 1024], BF16)

    def bank():
        b = banks[P.nbank % 6]
        P.nbank += 1
        return b

    class Arena:
        def __init__(self, buf):
            self.buf = buf
            self.off = 0
            self.gen = 0

        def reset(self):
            self.off = 0
            self.gen += 1

        def carve(self, name, cols):
            b = Buf(self.buf.t[:, self.off:self.off + cols], f"{name}@{self.gen}")
            self.off += cols
            assert self.off <= self.buf.t.shape[1], (name, self.off)
            return b

    AFa = Arena(arF)
    ABa = Arena(arB)

    def new_phase():
        P.barrier("S_wc")
        AFa.reset()
        ABa.reset()

    s_m = [P.new_sem(f"S_misc{i}") for i in range(6)]
    s_ot = P.new_sem("S_ot")
    s_ob = P.new_sem("S_ob")
    s_obl = P.new_sem("S_obl")
    s_wa = [P.new_sem("S_wa0"), P.new_sem("S_wa1")]
    s_wc = [[P.new_sem(f"S_wc{l}{g}") for g in "AB"] for l in range(depth)]

    def wgrp(n):
        return 0 if n.startswith("f1") else 1

    def convert(l, grp):
        for n, r, c in wnames:
            if wgrp(n) != grp:
                continue
            for r0 in range(0, r, 128):
                P.dma("pool", V(wb[n].t[l, r0:r0 + 128, :], (("wb", l, grp),)), V(wf[n].t[l, r0:r0 + 128, :], ()),
                      s_wc[l][grp], max_dma_last_dim=4096)

    convert(0, 0)
    convert(0, 1)

    def finish():
        P.barrier()
        P.close()
        return nc, P

    if stop_after == "conv":
        return finish()
    P.dma("sp", spar[:], spard[:], s_m[0])
    wsf = AFa.carve("wsf", depth * 512)
    P.dma("sp", wsf[:], wsTd[:], s_m[1])
    P.dma("sp", cst[:], cstd[:], s_m[2])
    P.dma("sp", cv[:], cvd[:], s_m[3])
    P.copy("dve", wsb[:], wsf[:])
    P.copy("dve", identb[:], cst[:, C_ID:C_ID + 128])
    P.copy("dve", onesb[:], cst[:, C_ONE:C_ONE + 128])
    P.memset("dve", psS[:], 0.0)
    P.act(scv[:], cv[:], AF.Silu)

    for d, nm in ((0, "lbf"), (1, "lbb")):
        raw = V(spar.t[:, soff[nm]:soff[nm] + 4 * depth].rearrange("p (h l) -> p h l", l=depth), spar[:].keys)
        P.reduce(lbm[:], raw, ALU.max)
        mb_ = V(bc(lbm.t[:, 0:1], [[1, 4], [0, depth]]), lbm[:].keys)
        P.tt("dve", lbtmp[:], raw, mb_, ALU.subtract)
        P.act(lbtmp[:], lbtmp[:], AF.Exp)
        P.reduce(lbm[:], lbtmp[:], ALU.add)
        P.recip(lbm[:], lbm[:])
        P.tt("dve", lbtmp[:], lbtmp[:], mb_, ALU.mult)
        P.memset("dve", lbt[d][:, :, 0:1], 0.0)
        for l in range(1, depth):
            P.tt("dve", lbt[d][:, :, l:l + 1], lbt[d][:, :, l - 1:l], lbtmp[:, :, l:l + 1], ALU.add)
        P.ts("dve", omlt[d][:], lbt[d][:], -1.0, ALU.mult, 1.0, ALU.add)

    if stop_after == "lb":
        return finish()
    tiles = [(i * TT, TT, 0) for i in range(NT)] + [(L, NCTX, 1)]

    P.dma("sp", V(xs.t[:, :, L:L + NCTX], (("xs", L),)), ctxT[:], s_m[4])
    for (t0, T, col) in tiles[:NT]:
        xa, sa = xring.next()
        xb_, sb_ = xring.next()
        P.dma("sp", xa[:], V(xT.t[:, :, t0:t0 + T], ()), sa)
        P.dma("sp", xb_[:], V(posT.t[:, :, t0:t0 + T], ()), sb_)
        P.tt("dve", xa[:], xa[:], xb_[:], ALU.add)
        P.dma("pool", xs_v(t0, T), xa[:], xring.ssem[xa.name])

    if stop_after == "pro":
        return finish()

    def mdv(j, col, kc=None):
        if kc is None:
            return V(md.t[:, j, :, col], md[:].keys)
        return V(md.t[:, j, kc, col:col + 1], md[:].keys)

    def sp_col(name, idx):
        o = soff[name] + idx
        return V(spar.t[:, o:o + 1], spar[:].keys)

    def norm_h(xt, T, gs_j, sh_j, col):
        pb = bank()
        for kc in range(DC):
            sq_ = V(sqb.t[:, kc % 2, :T], (("sqb", kc % 2),))
            if kc % 2 == 0:
                P.act(sq_, xt[:, kc, :T], AF.Square)
            else:
                P.tt("dve", sq_, xt[:, kc, :T], xt[:, kc, :T], ALU.mult)
            P.mm(pb[:, :T], onesb[:], sq_, start=(kc == 0), stop=(kc == DC - 1))
        P.act(rr[:, :T], pb[:, :T], AF.Ln, bias=eps_t[:, 0:1], scale=1.0 / D)
        P.act(rstd[:, :T], rr[:, :T], AF.Exp, scale=-0.5)
        for kc in range(DC):
            tm = tmpn2[kc % 2]
            P.tt("dve", tm[:, :T], xt[:, kc, :T], rstd[:, :T], ALU.mult)
            if gs_j is None:
                continue
            P.act(hT[:, kc, :T], tm[:, :T], AF.Identity, bias=mdv(sh_j, col, kc), scale=mdv(gs_j, col, kc))

    eps_t = P.sbuf("eps_t", [128, 2], F32)
    P.memset("dve", eps_t[:, 0:1], EPS)
    P.memset("dve", eps_t[:, 1:2], 1.0)

    def slab_load(src_ap, view_fn, l):
        def ld():
            sb, ss = slabs.next()
            P.dma("sp", V(view_fn(sb.t), (sb.name,)), V(src_ap, (("wb", l, 1),)), ss)
            return sb
        return ld

    def modulation(l):
        new_phase()
        wsl = [AFa.carve(f"wada{i}", DC * 384) for i in range(2)]
        pm = bank()
        P.dma("sp", bpar[:], V(bpard.t[l], ()), s_m[5])
        for s in range(24):
            wa = wsl[s % 2]
            wav = V(wa.t[:, :].rearrange("p (k n) -> p k n", n=384), (wa.name,))
            P.dma("sp", wav, V(w_ada.t[l, :, s * 384:(s + 1) * 384].rearrange("(k p) n -> p k n", p=128), ()), s_wa[s % 2])
            for jj in range(3):
                jc = s * 3 + jj
                for kc in range(DC):
                    P.mm(pm[:, jc * 2:jc * 2 + 2], V(wa.t[:, kc * 384 + jj * 128:kc * 384 + (jj + 1) * 128], (wa.name,)),
                         scv[:, kc, :], start=(kc == 0), stop=(kc == DC - 1))
        bo = soff["bada"] + l * 72
        P.tt("dve", mod[:], V(pm.t[:, 0:144].rearrange("p (j c) -> p j c", c=2), (pm.name,)),
             V(bc(spar.t[:, bo:bo + 1], [[1, 72], [0, 2]]), spar[:].keys), ALU.add)
        for sub in range(3):
            gcol = soff["g"] + l * 24 + sub * 8
            for col in range(2):
                g_ = V(spar.t[:, gcol:gcol + 8], spar[:].keys)
                P.copy("dve", mdv(sub * 3 + 0, col), V(mod.t[:, (sub * 3) * 8:(sub * 3) * 8 + 8, col], mod[:].keys))
                P.stt(mdv(sub * 3 + 1, col), V(mod.t[:, (sub * 3 + 1) * 8:(sub * 3 + 1) * 8 + 8, col], mod[:].keys), 1.0, g_,
                      ALU.add, ALU.mult)
                P.ts("dve", mdv(sub * 3 + 2, col), V(mod.t[:, (sub * 3 + 2) * 8:(sub * 3 + 2) * 8 + 8, col], mod[:].keys),
                     1.0 if sub == 1 else 0.5, ALU.mult)

    def ffn_pass(l, which, tlist):
        new_phase()
        w1, w3, w2 = (wb[f"f{which}w1"], wb[f"f{which}w3"], wb[f"f{which}w2"])
        wg = 0 if which == 1 else 1
        jsh, jgs, jhg = ((0, 1, 2) if which == 1 else (6, 7, 8))
        aT = ABa.carve("aT", FCN * TT)
        stmp = [AFa.carve(f"stmp{i}", TT) for i in range(2)]
        jobs = []
        state = {}
        xt_of = {}
        normed = set()

        def load_x(ti):
            if ti in xt_of:
                return
            t0_, T_, _ = tlist[ti]
            xt, sx = xring.next()
            P.dma("sp", xt[:, :, :T_], xs_v(t0_, T_), sx)
            xt_of[ti] = xt

        def do_norm(ti):
            if ti in normed:
                return
            load_x(ti)
            _, T_, col_ = tlist[ti]
            norm_h(xt_of[ti], T_, jgs, jsh, col_)
            normed.add(ti)

        for ti, (t0, T, col) in enumerate(tlist):
            def stage1(fb2, t0=t0, T=T, col=col, first=False, ti=ti):
                def ld():
                    if first:
                        load_x(ti)
                    xt = None
                    sb, ss = slabs.next()
                    for m, w in enumerate((w1, w3)):
                        dst = sb.t[:, m * 2048:(m + 1) * 2048].rearrange("p (k n) -> p k n", n=256)
                        src = w.t[l, :, fb2 * 256:(fb2 + 1) * 256].rearrange("(k p) n -> p k n", p=128)
                        P.dma("sp", V(dst, (sb.name,)), V(src, (("wb", l, wg),)), ss)
                    return (xt, sb)

                def cp(hd):
                    xt, sb = hd
                    if first:
                        do_norm(ti)
                    for half in range(2):
                        fblk = fb2 * 2 + half
                        p1, p3 = bank(), bank()
                        for m, pbk in ((0, p1), (1, p3)):
                            for kc in range(DC):
                                o = m * 2048 + kc * 256 + half * 128
                                P.mm(pbk[:, :T], V(sb.t[:, o:o + 128], (sb.name,)), hT[:, kc, :T],
                                     start=(kc == 0), stop=(kc == DC - 1))
                        st = stmp[fblk % 2]
                        P.act(st[:, :T], p1[:, :T], AF.Silu)
                        P.tt("dve", V(aT.t[:, fblk * TT:fblk * TT + T], (aT.name,)), st[:, :T], p3[:, :T], ALU.mult)
                return (ld, cp)

            def stage2(dh, fg, t0=t0, T=T, col=col, ti=ti):
                f0 = fg * 8
                nf = min(8, FCN - f0)

                def ld():
                    sb, ss = slabs.next()
                    dst = sb.t[:, :nf * 512].rearrange("p (k n) -> p k n", n=512)
                    src = w2.t[l, f0 * 128:(f0 + nf) * 128, dh * 512:(dh + 1) * 512].rearrange("(k p) n -> p k n", p=128)
                    P.dma("sp", V(dst, (sb.name,)), V(src, (("wb", l, wg),)), ss)
                    return sb

                def cp(sb):
                    if fg == 0 and dh == 0 and ti + 1 < len(tlist):
                        do_norm(ti + 1)
                    if fg == 0:
                        state["acc"] = [bank() for _ in range(4)]
                    acc = state["acc"]
                    for fl in range(nf):
                        fc = f0 + fl
                        for j in range(4):
                            P.mm(acc[j][:, :T], V(sb.t[:, fl * 512 + j * 128:fl * 512 + (j + 1) * 128], (sb.name,)),
                                 V(aT.t[:, fc * TT:fc * TT + T], (aT.name,)), start=(fc == 0), stop=(fc == FCN - 1))
                    if f0 + nf == FCN:
                        xt = xt_of[ti]
                        for j in range(4):
                            dd = dh * 4 + j
                            P.stt(xt[:, dd, :T], acc[j][:, :T], mdv(jhg, col, dd), xt[:, dd, :T], ALU.mult, ALU.add)
                        if dh == 1:
                            P.dma("pool", xs_v(t0, T), xt[:, :, :T], xring.ssem[xt.name])
                return (ld, cp)

            for fb2 in range(11):
                jobs.append(stage1(fb2, first=(fb2 == 0)))
            for dh in range(2):
                for fg in range(3):
                    jobs.append(stage2(dh, fg))
        run_jobs(jobs, 3)

    def mixer_pass(l, d, tlist):
        new_phase()
        win, wout = wb["win"], wb["wout"]
        qs = AFa.carve("qs", 4 * TT)
        fr4 = [AFa.carve(f"fr{i}", TT) for i in range(4)]
        lf = AFa.carve("lf", TT)
        kk = AFa.carve("kk", TT)
        bb = AFa.carve("bb", TT)
        a16 = AFa.carve("a16", TT)
        Ab = AFa.carve("Ab", TT)
        aB = AFa.carve("aB", TT)
        ee = AFa.carve("ee", TT)
        tD = [AFa.carve(f"tD{i}", TT) for i in range(2)]
        x1 = AFa.carve("x1", TT)
        dn2 = [AFa.carve(f"dn{i}", 8) for i in range(2)]
        S_all = AFa.carve("S_all", 9 * 128)
        obuf = AFa.carve("obuf", 4 * TT)
        if d == 0:
            gv = AFa.carve("gv", 512)
            junk = AFa.carve("junk", 128)
            ssv = AFa.carve("ssv", 16)
            tg = AFa.carve("tg", TT)
        v_tm = ABa.carve("v_tm", 4 * 512)
        qt2 = [ABa.carve(f"qt{j}", TT) for j in range(2)]
        qb2 = [ABa.carve(f"qb{j}", TT) for j in range(2)]
        kt2 = [[ABa.carve(f"kt{j}{i}", TT) for i in range(4)] for j in range(2)]
        for j in range(2):
            for i in range(4):
                P.memset("pool", kt2[j][i][:, :], 0.0)
        ke2 = [ABa.carve(f"ke{j}", TT) for j in range(2)]
        ke_tm = [ABa.carve(f"ke_tm{i}", TT) for i in range(2)]
        for i in range(2):
            P.memset("dve", ke_tm[i][:, :], 0.0)
        Sb = ABa.carve("Sb", 8 * 128)
        PT = ABa.carve("PT", TT)
        if d == 0:
            vn_tm = ABa.carve("vn_tm", 4 * 512)
            mixT = ABa.carve("mixT", 8 * TT)
            osq = ABa.carve("osq", TT)
            gu = ABa.carve("gu", 4 * TT)
            sog = ABa.carve("sog", 4 * TT)
        m64 = V(cst.t[:, C_M64:C_M64 + 512], cst[:].keys)
        m16 = V(cst.t[:, C_M16:C_M16 + 512], cst[:].keys)
        cmask = V(cst.t[:, (C_MF if d == 0 else C_MB):(C_MF if d == 0 else C_MB) + 512], cst[:].keys)
        state = {}
        xt_of = {}
        normed = set()

        def load_x(ti):
            if ti in xt_of:
                return
            t0_, T_, _ = tlist[ti]
            xt, sx = xring.next()
            P.dma("sp", xt[:, :, :T_], xs_v(t0_, T_), sx)
            xt_of[ti] = xt

        def do_norm(ti):
            if ti in normed:
                return
            load_x(ti)
            _, T_, col_ = tlist[ti]
            norm_h(xt_of[ti], T_, 4, 3, col_)
            normed.add(ti)

        def c3(buf, T, s=64):
            return V(buf.t[:, :T].rearrange("p (n s) -> p n s", s=s), (buf.name,))

        def scan_E(hd, T, frb, st):
            NCH, NP = T // 64, T // 128
            qt, qb, kt, ke, dn = qt2[st], qb2[st], kt2[st], ke2[st], dn2[st]
            qsh = V(qs.t[:, hd * TT:hd * TT + T], (qs.name,))
            fr = fr4[hd]
            P.ts("dve", fr[:, :T], fr[:, :T], V(omlt[d].t[:, hd, l:l + 1], omlt[d][:].keys), ALU.mult,
                 V(lbt[d].t[:, hd, l:l + 1], lbt[d][:].keys), ALU.add)
            P.act(lf[:, :T], fr[:, :T], AF.Ln)
            P.ts("dve", kk[:, :T], fr[:, :T], -1.0, ALU.mult, 1.0, ALU.add)
            P.scan(bb[:, :T], V(m64.ap[:, :T], m64.keys), lf[:, :T], 0.0, ALU.mult, ALU.add)
            P.scan(a16[:, :T], V(m16.ap[:, :T], m16.keys), lf[:, :T], 0.0, ALU.mult, ALU.add)
            bend = V(bc(bb.t[:, 63:64], [[64, NCH], [0, 64]]), (bb.name,))
            if d == 0:
                A, a = bb, a16
            else:
                P.tt("dve", c3(Ab, T), bend, c3(bb, T), ALU.subtract)
                P.tt("dve", Ab[:, :T], Ab[:, :T], lf[:, :T], ALU.add)
                tot = V(bc(a16.t[:, 15:16], [[16, T // 16], [0, 16]]), (a16.name,))
                P.tt("dve", c3(aB, T, 16), tot, c3(a16, T, 16), ALU.subtract)
                P.tt("dve", aB[:, :T], aB[:, :T], lf[:, :T], ALU.add)
                A, a = Ab, aB
            def c3s(buf, r0, r1):
                return V(buf.t[:, :T].rearrange("p (n s) -> p n s", s=64)[:, :, r0:r1], (buf.name,))

            P.act(a[:, :T], a[:, :T], AF.Exp)
            P.act(ee[:, :T], A[:, :T], AF.Exp)
            P.tt("pool", qt[:, :T], qsh, a[:, :T], ALU.mult)
            P.tt("pool", qb[:, :T], qsh, ee[:, :T], ALU.mult)
            for i in range(4):
                r0, r1 = (0, 16 * (i + 1)) if d == 0 else (16 * i, 64)
                tDj = tD[i % 2]
                if (d == 0 and i == 0) or (d == 1 and i == 3):
                    P.act(c3s(tDj, r0, r1), c3s(A, r0, r1), AF.Exp, scale=-1.0)
                else:
                    colr = 16 * i - 1 if d == 0 else 16 * (i + 1)
                    refb = V(bc(A.t[:, colr:colr + 1], [[64, NCH], [0, r1 - r0]]), (A.name,))
                    P.tt("dve", c3s(tDj, r0, r1), c3s(A, r0, r1), refb, ALU.subtract)
                    P.act(c3s(tDj, r0, r1), c3s(tDj, r0, r1), AF.Exp, scale=-1.0)
                P.tt("pool", c3s(kt[i], r0, r1), c3s(kk, r0, r1), c3s(tDj, r0, r1), ALU.mult)
            if d == 0:
                P.tt("dve", c3(x1, T), bend, c3(bb, T), ALU.subtract)
            else:
                P.tt("dve", x1[:, :T], bb[:, :T], lf[:, :T], ALU.subtract)
            P.act(x1[:, :T], x1[:, :T], AF.Exp)
            P.tt("pool", ke[:, :T], kk[:, :T], x1[:, :T], ALU.mult)
            P.act(dn[:, :NCH], V(bb.t[:, 63:T:64], (bb.name,)), AF.Exp)

        def scan_M(hd, T, st):
            NCH, NP = T // 64, T // 128
            order = list(range(NCH)) if d == 0 else list(range(NCH - 1, -1, -1))
            qt, qb, kt, ke, dn = qt2[st], qb2[st], kt2[st], ke2[st], dn2[st]
            for p in range(NP):
                P.transpose(psT[:, p * 128:(p + 1) * 128], ke[:, p * 128:(p + 1) * 128], identb[:])
            P.copy("act", V(ke_tm[0].t[0:64, :T], (ke_tm[0].name,)), V(psT.t[0:64, :T], (psT.name,)))
            P.copy("act", V(ke_tm[1].t[64:128, :T], (ke_tm[1].name,)), V(psT.t[64:128, :T], (psT.name,)))
            ub = [bank() for _ in range((NCH + 3) // 4)]
            for n in range(NCH):
                p, hf = n // 2, n % 2
                P.mm(ub[n // 4][:, (n % 4) * 128:(n % 4 + 1) * 128],
                     V(ke_tm[hf].t[:, p * 128:(p + 1) * 128], (ke_tm[hf].name,)),
                     V(v_tm.t[:, p * 512 + hd * 128:p * 512 + (hd + 1) * 128], (v_tm.name,)))
            P.copy("dve", S_all[:, 0:128], Sst[d][hd][:])
            for j, n in enumerate(order):
                P.stt(S_all[:, (j + 1) * 128:(j + 2) * 128], S_all[:, j * 128:(j + 1) * 128], dn[:, n:n + 1],
                      ub[n // 4][:, (n % 4) * 128:(n % 4 + 1) * 128], ALU.mult, ALU.add)
            P.copy("dve", Sst[d][hd][:], S_all[:, NCH * 128:(NCH + 1) * 128])
            P.copy("act", Sb[:, :NCH * 128], S_all[:, :NCH * 128])
            for n in range(NCH):
                p, hf = n // 2, n % 2
                for i in range(4):
                    c0 = p * 128 + hf * 64 + 16 * i
                    P.mm(V(psS.t[hf * 64:(hf + 1) * 64, c0:c0 + 16], (psS.name,)),
                         kt[i][:, n * 64:(n + 1) * 64], qt[:, n * 64 + 16 * i:n * 64 + 16 * i + 16])
            P.tt("dve", PT[:, :T], psS[:, :T], V(cmask.ap[:, :T], cmask.keys), ALU.mult)
            po = bank()
            first = True
            for p in range(NP):
                P.mm(po[:, p * 128:(p + 1) * 128],
                     V(v_tm.t[:, p * 512 + hd * 128:p * 512 + (hd + 1) * 128], (v_tm.name,)),
                     PT[:, p * 128:(p + 1) * 128], start=first, stop=False, skip_group_check=True)
                first = False
                for n in (2 * p, 2 * p + 1):
                    j = order.index(n)
                    P.mm(po[:, n * 64:(n + 1) * 64], Sb[:, j * 128:(j + 1) * 128], qb[:, n * 64:(n + 1) * 64],
                         start=False, stop=(n == NCH - 1), skip_group_check=True)
            oh = V(obuf.t[:, hd * TT:hd * TT + T], (obuf.name,))
            if d == 1:
                P.copy("act", oh, po[:, :T])
            else:
                P.tt("dve", oh, po[:, :T], oh, ALU.add)

        def sec_load(c0):
            def view(t):
                return t[:, :].rearrange("p (k n) -> p k n", n=512)
            return slab_load(win.t[l, :, c0:c0 + 512].rearrange("(k p) n -> p k n", p=128), view, l)

        jobs = []
        for ti, (t0, T, col) in enumerate(tlist):
            NP = T // 128

            def proj_fm(sb, T, evac):
                for hh in range(4):
                    pb = bank()
                    for kc in range(DC):
                        o = kc * 512 + hh * 128
                        P.mm(pb[:, :T], V(sb.t[:, o:o + 128], (sb.name,)), hT[:, kc, :T], start=(kc == 0), stop=(kc == DC - 1))
                    evac(hh, pb)

            def proj_tm(sb, T, evac):
                for p in range(T // 128):
                    pb = bank()
                    for kc in range(DC):
                        P.mm(pb[:, :], hT[:, kc, p * 128:(p + 1) * 128], V(sb.t[:, kc * 512:(kc + 1) * 512], (sb.name,)),
                             start=(kc == 0), stop=(kc == DC - 1))
                    evac(p, pb)

            def job_q(t0=t0, T=T, col=col, ti=ti):
                sl = sec_load(1024)

                def ld():
                    load_x(ti)
                    xt = None
                    if d == 0:
                        P.dma("sp", V(obuf.t[:, :].rearrange("p (h t) -> p h t", t=TT)[:, :, :T], (obuf.name,)), ob_v(t0, T), s_obl)
                    return (xt, sl())

                def cp(hd_):
                    xt, sb = hd_
                    do_norm(ti)
                    proj_fm(sb, T, lambda hh, pb: P.act(V(qs.t[:, hh * TT:hh * TT + T], (qs.name,)), pb[:, :T], AF.Silu))
                return (ld, cp)

            def job_i(T=T):
                def cp(sb):
                    proj_tm(sb, T, lambda p, pb: P.copy("act", V(v_tm.t[:, p * 512:(p + 1) * 512], (v_tm.name,)), pb[:, :]))
                return (sec_load(2560), cp)

            def job_f(t0=t0, T=T, ti=ti):
                def cp(sb):
                    proj_fm(sb, T, lambda hh, pb: P.act(fr4[hh][:, :T], pb[:, :T], AF.Sigmoid))
                    if ti + 1 < len(tlist):
                        do_norm(ti + 1)

                    def stage_E(hh):
                        scan_E(hh, T, None, hh % 2)
                    stage_E(0)
                    for hh in range(4):
                        if hh + 1 < 4:
                            stage_E(hh + 1)
                        scan_M(hh, T, hh % 2)
                    if d == 1:
                        P.dma("pool", ob_v(t0, T), V(obuf.t[:, :].rearrange("p (h t) -> p h t", t=TT)[:, :, :T], (obuf.name,)), s_ob)
                return (sec_load(1536 if d == 0 else 2048), cp)

            def job_u(T=T):
                def cp(sb):
                    proj_fm(sb, T, lambda hh, pb: P.act(V(gu.t[:, hh * TT:hh * TT + T], (gu.name,)), pb[:, :T], AF.Gelu_apprx_tanh))
                return (sec_load(0), cp)

            def job_og(T=T):
                def cp(sb):
                    proj_fm(sb, T, lambda hh, pb: P.act(V(sog.t[:, hh * TT:hh * TT + T], (sog.name,)), pb[:, :T], AF.Silu))
                return (sec_load(3072), cp)

            def job_v(T=T):
                def cp(sb):
                    gvb = V(bpar.t[:, 0:512], bpar[:].keys)

                    def ev(p, pb, gv0=gv):
                        gv = (gv0, x1)[p % 2]
                        P.act(gv[:, :], pb[:, :], AF.Gelu_apprx_tanh)
                        for g in range(4):
                            P.act(junk[:, :], gv[:, g * 128:(g + 1) * 128], AF.Square, accum_out=ssv[:, p * 4 + g:p * 4 + g + 1])
                        P.act(ssv[:, p * 4:p * 4 + 4], ssv[:, p * 4:p * 4 + 4], AF.Ln, bias=eps_t[:, 0:1], scale=1.0 / 128)
                        P.act(ssv[:, p * 4:p * 4 + 4], ssv[:, p * 4:p * 4 + 4], AF.Exp, scale=-0.5)
                        for g in range(4):
                            P.stt(V(vn_tm.t[:, p * 512 + g * 128:p * 512 + (g + 1) * 128], (vn_tm.name,)), gv[:, g * 128:(g + 1) * 128],
                                  ssv[:, p * 4 + g:p * 4 + g + 1], V(gvb.ap[:, g * 128:(g + 1) * 128], gvb.keys), ALU.mult, ALU.mult)
                    proj_tm(sb, T, ev)
                    for g in range(4):
                        pb = bank()
                        for p in range(T // 128):
                            P.mm(pb[:, p * 128:(p + 1) * 128], V(vn_tm.t[:, p * 512 + g * 128:p * 512 + (g + 1) * 128], (vn_tm.name,)),
                                 V(wsb.t[:, l * 512 + g * 128:l * 512 + (g + 1) * 128], wsb[:].keys))
                        P.tt("dve", tg[:, :T], pb[:, :T], V(bpar.t[:, 512 + g * 512:512 + g * 512 + T], bpar[:].keys), ALU.add)
                        P.tt("dve", V(mixT.t[:, g * TT:g * TT + T], (mixT.name,)), tg[:, :T], V(gu.t[:, g * TT:g * TT + T], (gu.name,)), ALU.mult)
                return (sec_load(512), cp)

            def job_comb(T=T):
                def cp(_):
                    sqs = [qt2[0], qt2[1], qb2[0], qb2[1]]
                    rrs = [lf, kk, bb, a16]
                    tgs = [Ab, aB, ee, x1]
                    for hd in range(4):
                        oh = V(obuf.t[:, hd * TT:hd * TT + T], (obuf.name,))
                        P.act(sqs[hd][:, :T], oh, AF.Square)
                        pb = bank()
                        P.mm(pb[:, :T], onesb[:], sqs[hd][:, :T])
                        P.act(rrs[hd][:, :T], pb[:, :T], AF.Ln, bias=eps_t[:, 0:1], scale=1.0 / 128)
                        P.act(rrs[hd][:, :T], rrs[hd][:, :T], AF.Exp, scale=-0.5)
                        P.tt("dve", tgs[hd][:, :T], oh, rrs[hd][:, :T], ALU.mult)
                        P.stt(V(mixT.t[:, (4 + hd) * TT:(4 + hd) * TT + T], (mixT.name,)), tgs[hd][:, :T], sp_col("go", l * 4 + hd),
                              V(sog.t[:, hd * TT:hd * TT + T], (sog.name,)), ALU.mult, ALU.mult)
                return (None, cp)

            def job_out(half, t0=t0, T=T, col=col, ti=ti):
                def view(t):
                    return t[:, :].rearrange("p (k n) -> p k n", n=512)

                def cp(sb):
                    xt = xt_of[ti]
                    for j in range(4):
                        dd = half * 4 + j
                        pb = bank()
                        for kc in range(DC):
                            o = kc * 512 + j * 128
                            P.mm(pb[:, :T], V(sb.t[:, o:o + 128], (sb.name,)), V(mixT.t[:, kc * TT:kc * TT + T], (mixT.name,)),
                                 start=(kc == 0), stop=(kc == DC - 1))
                        P.stt(xt[:, dd, :T], pb[:, :T], mdv(5, col, dd), xt[:, dd, :T], ALU.mult, ALU.add)
                    if half == 1:
                        P.dma("pool", xs_v(t0, T), xt[:, :, :T], xring.ssem[xt.name])
                return (slab_load(wout.t[l, :, half * 512:(half + 1) * 512].rearrange("(k p) n -> p k n", p=128), view, l), cp)

            jobs.append(job_q())
            jobs.append(job_i())
            if d == 0:
                jobs.append(job_u())
                jobs.append(job_og())
                jobs.append(job_v())
            jobs.append(job_f())
            if d == 0:
                jobs.append(job_comb())
                jobs.append(job_out(0))
                jobs.append(job_out(1))
        run_jobs(jobs, 3)

    def final_norm():
        new_phase()
        ot = AFa.carve("ot", DC * TT)
        for (t0, T, col) in tiles[:NT]:
            xt, sx = xring.next()
            P.dma("sp", xt[:, :, :T], xs_v(t0, T), sx)
            pb = bank()
            for kc in range(DC):
                sq_ = V(sqb.t[:, kc % 2, :T], (("sqb", kc % 2),))
                P.act(sq_, xt[:, kc, :T], AF.Square)
                P.mm(pb[:, :T], onesb[:], sq_, start=(kc == 0), stop=(kc == DC - 1))
            P.act(rr[:, :T], pb[:, :T], AF.Ln, bias=eps_t[:, 0:1], scale=1.0 / D)
            P.act(rstd[:, :T], rr[:, :T], AF.Exp, scale=-0.5)
            for kc in range(DC):
                P.stt(V(ot.t[:, kc * TT:kc * TT + T], (ot.name,)), xt[:, kc, :T], sp_col("gfin", kc), rstd[:, :T], ALU.mult, ALU.mult)
            P.dma("pool", V(outT.t[:, :, t0:t0 + T], ()), V(ot.t[:, :].rearrange("p (k t) -> p k t", t=TT)[:, :, :T], (ot.name,)), s_ot)

    lat_tiles = tiles[:NT]
    ctx_tile = tiles[NT:]
    for l in range(depth):
        modulation(l)
        if l + 1 < depth:
            convert(l + 1, 0)
            convert(l + 1, 1)
        if stop_after == "mod":
            break
        ffn_pass(l, 1, tiles)
        if stop_after == "ffn1":
            break
        for dd in range(2):
            for hd in range(4):
                P.memset("dve", Sst[dd][hd][:], 0.0)
        mixer_pass(l, 1, ctx_tile)
        if stop_after == "mB":
            break
        mixer_pass(l, 0, ctx_tile)
        if stop_after == "mF":
            break
        mixer_pass(l, 1, lat_tiles[::-1])
        mixer_pass(l, 0, lat_tiles)
        if stop_after == "mix":
            break
        ffn_pass(l, 2, tiles if l < depth - 1 else lat_tiles)
    if stop_after is None:
        final_norm()
    P.barrier()
    P.close()
    return nc, P


def prep_inputs(inp, L, depth, nb):
    f32 = lambda a: np.ascontiguousarray(np.asarray(a, dtype=np.float32))
    soff, NSP = spar_layout(depth)
    spar = np.zeros((128, NSP), np.float32)
    g = np.stack([f32(inp["norm_ffn1_g"])[:depth], f32(inp["norm_mix_g"])[:depth], f32(inp["norm_ffn2_g"])[:depth]], axis=1)
    spar[:, soff["g"]:soff["g"] + depth * 24] = fmv(g).reshape(128, depth * 24)
    spar[:, soff["gfin"]:soff["gfin"] + 8] = fmv(f32(inp["norm_final_g"]))
    spar[:, soff["go"]:soff["go"] + depth * 4] = fmv(f32(inp["hgrn_norm_g"])[:depth]).reshape(128, depth * 4)
    for nm, key in (("lbf", "hgrn_lb_fwd"), ("lbb", "hgrn_lb_bwd")):
        a = fmv(f32(inp[key])[:depth])
        spar[:, soff[nm]:soff[nm] + 4 * depth] = a.transpose(0, 2, 1).reshape(128, 4 * depth)
    spar[:, soff["bada"]:soff["bada"] + depth * 72] = fmv(f32(inp["b_ada"])[:depth]).reshape(128, depth * 72)
    bpar = np.zeros((depth, 128, 512 + 2048), np.float32)
    gng = f32(inp["gmlp_norm_g"])
    gbs = f32(inp["gmlp_bs"])
    for l in range(depth):
        bpar[l, :, 0:512] = gng[l][None, :]
        bpar[l, :, 512:] = np.tile(gbs[l][:, None, :], (1, 4, 1)).reshape(1, 2048)
    ws = f32(inp["gmlp_ws"])[:depth]
    wsT = np.ascontiguousarray(ws.transpose(3, 0, 1, 2)).reshape(128, depth * 512)
    cst = make_consts()
    pos = pos_table(L)
    shared = {
        "posT": pos, "spar": spar, "bpar": bpar, "wsT": wsT, "cst": cst,
        "w_ada": f32(inp["w_ada"])[:depth],
        "f1w1": f32(inp["ffn1_w1"])[:depth], "f1w3": f32(inp["ffn1_w3"])[:depth], "f1w2": f32(inp["ffn1_w2"])[:depth],
        "f2w1": f32(inp["ffn2_w1"])[:depth], "f2w3": f32(inp["ffn2_w3"])[:depth], "f2w2": f32(inp["ffn2_w2"])[:depth],
        "win": f32(inp["w_in"])[:depth], "wout": f32(inp["w_out"])[:depth],
    }
    x = f32(inp["x"])
    ctx = f32(inp["ctx"])
    c = f32(inp["c"])
    cc = f32(inp["c_ctx"])
    maps = []
    for b in range(nb):
        m = dict(shared)
        m["xT"] = fm(x[b])
        m["ctxT"] = fm(ctx[b])
        m["cv"] = np.ascontiguousarray(np.stack([fmv(c[b]), fmv(cc)], axis=-1))
        maps.append(m)
    return maps


def run(inp, L, depth, nb, trace=False):
    nc, P = build(L, depth)
    maps = prep_inputs(inp, L, depth, nb)
    res = run_bass_kernel_spmd(nc, maps, core_ids=list(range(nb)))
    outs = []
    for b in range(nb):
        o = res.results[b]["outT"]
        outs.append(np.ascontiguousarray(o.transpose(2, 1, 0).reshape(L, D)))
    return np.stack(outs, axis=0)


def kernel(**inputs):
    x = np.asarray(inputs["x"])
    B, L, _ = x.shape
    depth = np.asarray(inputs["w_ada"]).shape[0]
    return run(inputs, L, depth, B).astype(np.float32)
```
